# Optimizing a Trainium2 kernel written in Bass

```python
import jax, jax.numpy as jnp
from jax import lax
import numpy as np

D_MODEL = 1024
BATCH = 8
SEQ = 2048
DEPTH = 1
DEC_BATCH = 128
DEC_SEQ = 1
PAST_LEN = 16384
PAGE_SIZE = 128

D_A = D_MODEL // 2
G_A = 4
GC_A = D_A // G_A
WINDOWS = (2, 4, 8, 16)
POOL_BUF = max(WINDOWS) - 1
D_B = D_MODEL // 2
G_B = 4
C_B = D_B // G_B
CHUNK = 128
D_IN = 2 * D_A + 3 * D_B + 2 * D_MODEL
EPS = 1e-6

kernel_name = "pool_sgu_hybrid_decoder_step"


def rmsnorm(x, g):
    xf = x.astype(jnp.float32)
    r = lax.rsqrt(jnp.mean(xf * xf, axis=-1, keepdims=True) + EPS)
    return (xf * r).astype(x.dtype) * g


def layernorm(x, g, b):
    xf = x.astype(jnp.float32)
    mu = jnp.mean(xf, axis=-1, keepdims=True)
    var = jnp.mean(jnp.square(xf - mu), axis=-1, keepdims=True)
    return ((xf - mu) * lax.rsqrt(var + EPS)).astype(x.dtype) * g + b


def pool_mixer(a, prefix, pos0, pool_w, pool_b, pool_scale):
    B, T, _ = a.shape
    P = POOL_BUF
    ext = jnp.concatenate([prefix, a], axis=1)
    cs = jnp.cumsum(ext.astype(jnp.float32), axis=1)
    cs0 = jnp.concatenate([jnp.zeros((B, 1, D_A), jnp.float32), cs], axis=1)
    pos = pos0 + jnp.arange(T)
    outs = []
    for k, w in enumerate(WINDOWS):
        sl = slice(k * GC_A, (k + 1) * GC_A)
        s = cs0[:, P + 1:P + 1 + T, sl] - cs0[:, P + 1 - w:P + 1 - w + T, sl]
        cnt = jnp.minimum(pos + 1, w).astype(jnp.float32)
        outs.append(s / cnt[None, :, None])
    pooled = jnp.concatenate(outs, axis=-1).astype(a.dtype) - a
    y = jnp.einsum('btgc,gcd->btgd', pooled.reshape(B, T, G_A, GC_A), pool_w) + pool_b
    return y.reshape(B, T, D_A) * pool_scale, ext[:, -P:]


def spatial_gate(u, v, ln_g, ln_b, sgu_w, sgu_b):
    u = jax.nn.gelu(u, approximate=False)
    v = layernorm(jax.nn.gelu(v, approximate=False), ln_g, ln_b)
    B, T, _ = v.shape
    Lc = min(T, CHUNK)
    n = T // Lc
    vr = v.reshape(B, n, Lc, G_B, C_B)
    tril = jnp.tril(jnp.ones((Lc, Lc), dtype=bool))
    W = jnp.where(tril[None], sgu_w[:, :Lc, :Lc], 0.0).astype(v.dtype)
    s = jnp.einsum('gts,bnsgc->bntgc', W, vr) + sgu_b[:, :Lc].T[:, :, None]
    return u * s.reshape(B, T, D_B), v[:, T - Lc:]


def layer(x, c, prefix, pos0, w_ada, b_ada, norm_gain, w_in, pool_w, pool_b, pool_scale,
          sgu_ln_g, sgu_ln_b, sgu_w, sgu_b, w_branch_a, w_branch_b, w_out):
    mod = jax.nn.silu(c) @ w_ada + b_ada
    shift, scale, gate = jnp.split(mod, 3, axis=-1)
    h = rmsnorm(x, norm_gain) * (1.0 + scale[:, None]) + shift[:, None]
    z = h @ w_in
    cuts = np.cumsum([D_A, D_A, D_B, D_B, D_B, D_MODEL])
    a, ga, u, v, gb, ma, mb = jnp.split(z, [int(i) for i in cuts], axis=-1)
    ya, new_pool = pool_mixer(a, prefix, pos0, pool_w, pool_b, pool_scale)
    ya = ya * jax.nn.silu(ga)
    yb, new_v = spatial_gate(u, v, sgu_ln_g, sgu_ln_b, sgu_w, sgu_b)
    yb = yb * jax.nn.silu(gb)
    merged = jax.nn.sigmoid(ma) * (ya @ w_branch_a) + jax.nn.sigmoid(mb) * (yb @ w_branch_b)
    x = x + gate[:, None] * (merged @ w_out)
    return x, new_pool, new_v


def setup_inputs(seed: int = 0) -> dict:
    key = jax.random.key(seed)
    ks = jax.random.split(key, 24)
    f = jnp.float32
    nrm = lambda k, s, sc: jax.random.normal(k, s, f) * sc
    return {
        "x_prompt": nrm(ks[0], (BATCH, SEQ, D_MODEL), 1.0),
        "x_sample": nrm(ks[1], (DEC_BATCH, DEC_SEQ, D_MODEL), 1.0),
        "state_pool": nrm(ks[2], (DEPTH, DEC_BATCH, POOL_BUF, D_A), 0.6),
        "c_prompt": nrm(ks[3], (BATCH, D_MODEL), 1.0),
        "c_sample": nrm(ks[4], (DEC_BATCH, D_MODEL), 1.0),
        "w_ada": nrm(ks[5], (DEPTH, D_MODEL, 3 * D_MODEL), 0.5 * D_MODEL ** -0.5),
        "b_ada": nrm(ks[6], (DEPTH, 3 * D_MODEL), 0.02),
        "norm_gain": 1.0 + nrm(ks[7], (DEPTH, D_MODEL), 0.05),
        "w_in": nrm(ks[8], (DEPTH, D_MODEL, D_IN), D_MODEL ** -0.5),
        "pool_w": nrm(ks[9], (DEPTH, G_A, GC_A, GC_A), GC_A ** -0.5),
        "pool_b": nrm(ks[10], (DEPTH, G_A, GC_A), 0.02),
        "pool_scale": 1.0 + nrm(ks[11], (DEPTH, D_A), 0.1),
        "sgu_ln_g": 1.0 + nrm(ks[12], (DEPTH, D_B), 0.05),
        "sgu_ln_b": nrm(ks[13], (DEPTH, D_B), 0.02),
        "sgu_w": nrm(ks[14], (DEPTH, G_B, CHUNK, CHUNK), CHUNK ** -0.5),
        "sgu_b": 1.0 + nrm(ks[15], (DEPTH, G_B, CHUNK), 0.1),
        "w_branch_a": nrm(ks[16], (DEPTH, D_A, D_MODEL), D_A ** -0.5),
        "w_branch_b": nrm(ks[17], (DEPTH, D_B, D_MODEL), D_B ** -0.5),
        "w_out": nrm(ks[18], (DEPTH, D_MODEL, D_MODEL), D_MODEL ** -0.5),
        "final_gain": 1.0 + nrm(ks[19], (D_MODEL,), 0.05),
    }


def reference(x_prompt, x_sample, state_pool, c_prompt, c_sample, w_ada, b_ada, norm_gain, w_in,
              pool_w, pool_b, pool_scale, sgu_ln_g, sgu_ln_b, sgu_w, sgu_b, w_branch_a, w_branch_b,
              w_out, final_gain):
    xp, xs = x_prompt, x_sample
    pool_p, pool_s, v_p, v_s = [], [], [], []
    zero_prefix = jnp.zeros((BATCH, POOL_BUF, D_A), x_prompt.dtype)
    for l in range(DEPTH):
        params = (w_ada[l], b_ada[l], norm_gain[l], w_in[l], pool_w[l], pool_b[l], pool_scale[l],
                  sgu_ln_g[l], sgu_ln_b[l], sgu_w[l], sgu_b[l], w_branch_a[l], w_branch_b[l], w_out[l])
        xp, npp, nvp = layer(xp, c_prompt, zero_prefix, 0, *params)
        xs, nps, nvs = layer(xs, c_sample, state_pool[l], PAST_LEN, *params)
        pool_p.append(npp); pool_s.append(nps); v_p.append(nvp); v_s.append(nvs)
    y_prompt = rmsnorm(xp, final_gain)
    y_sample = rmsnorm(xs, final_gain)
    return (y_prompt, y_sample, jnp.stack(pool_p), jnp.stack(pool_s), jnp.stack(v_p), jnp.stack(v_s))
```

```python
import numpy as np
import concourse.bass as bass
import concourse.mybir as mybir
from concourse.bass_utils import run_bass_kernel_spmd

F32, BF16 = mybir.dt.float32, mybir.dt.bfloat16
AF = mybir.ActivationFunctionType
ALU = mybir.AluOpType
AX = mybir.AxisListType

D = 1024
T = 2048
NS = 16
DA = 512
DIN = 4608
PBUF = 15
EPS = 1e-6
WINDOWS = (2, 4, 8, 16)
PW = 34
C_A, C_GA, C_U, C_V, C_GB, C_MA, C_MB = 0, 512, 1024, 1536, 2048, 2560, 3584
RING = 8
SBUF_BASE = 16512
SBUF_LIMIT = 229344


class Buf:
    __slots__ = ("name", "w", "r")

    def __init__(self, name):
        self.name = name
        self.w = None
        self.r = []


class Op:
    __slots__ = ("eng", "pos", "fn", "deps", "needed", "seq", "dma", "dsem", "dval", "dprev", "tag", "indep", "sraw")


class Sched:
    ENGS = ("pe", "act", "dve", "pool", "sp")

    def __init__(self, nc, ring=RING):
        self.nc = nc
        self.ops = {e: [] for e in self.ENGS}
        self.sem = {e: nc.alloc_semaphore("s_" + e) for e in self.ENGS}
        self.ring_n = ring
        self.ring = [nc.alloc_semaphore(f"d_sp_{i}") for i in range(ring)]
        self.ndma = {"sp": 0, "pool": 0, "act": 0}
        self.pool_sems = []
        self.log = None

    def add(self, eng, fn, reads=(), writes=(), dma=False, indep=False):
        op = Op()
        op.indep = indep
        op.tag = 0
        op.eng, op.fn, op.dma = eng, fn, dma
        op.pos = len(self.ops[eng])
        op.needed, op.seq = False, None
        op.dsem = op.dval = op.dprev = None
        deps = []
        op.sraw = None
        for b in list(reads) + list(writes):
            if b.w is not None and b.w.eng == eng and not b.w.dma and not dma:
                if op.sraw is None or op.sraw.pos < b.w.pos:
                    op.sraw = b.w
        if op.sraw is not None:
            op.sraw.needed = True
        for b in reads:
            if b.w is not None:
                deps.append(b.w)
        for b in writes:
            if b.w is not None:
                deps.append(b.w)
            deps.extend(b.r)
        seen, ud = set(), []
        for d in deps:
            if id(d) not in seen and d is not op:
                seen.add(id(d))
                ud.append(d)
        best = {}
        pr = []
        for d in ud:
            if d.dma:
                pr.append(d)
            elif d.eng not in best or best[d.eng].pos < d.pos:
                best[d.eng] = d
        ud = pr + list(best.values())
        op.deps = ud
        for d in ud:
            if not (d.eng == "pe" and eng == "pe" and not d.dma):
                d.needed = True
        for b in reads:
            b.r.append(op)
        for b in writes:
            b.w = op
            b.r = []
        if dma:
            i = self.ndma[eng]
            self.ndma[eng] += 1
            if eng == "sp":
                op.dsem = self.ring[i % self.ring_n]
                op.dval = 16 * (i // self.ring_n + 1)
                if i >= self.ring_n:
                    op.dprev = (op.dsem, 16 * (i // self.ring_n))
            else:
                s = self.nc.alloc_semaphore(f"d_{eng}_{i}")
                self.pool_sems.append(s)
                op.dsem, op.dval = s, 16
        self.ops[eng].append(op)
        return op

    def emit(self):
        nc = self.nc
        for e in self.ENGS:
            n = 0
            for op in self.ops[e]:
                if op.needed and not op.dma:
                    n += 1
                    op.seq = n
        engobj = {"pe": nc.tensor, "act": nc.scalar, "dve": nc.vector, "pool": nc.gpsimd, "sp": nc.sync}

        def run(ename, eng):
            waited = {}
            ops = self.ops[ename]
            for op in ops:
                need = {}
                for d in op.deps:
                    if d.dma:
                        key, val = d.dsem, d.dval
                    else:
                        if d.eng == ename:
                            if ename == "pe" or op.indep or op.dma is False:
                                continue
                            if op.dma and False:
                                continue
                        key, val = self.sem[d.eng], d.seq
                    if waited.get(key.num, 0) >= val:
                        continue
                    if need.get(key.num, (None, 0))[1] < val:
                        need[key.num] = (key, val)
                if op.sraw is not None and ename != "pe" and not op.indep and op.pos - op.sraw.pos <= 3:
                    key, val = self.sem[ename], op.sraw.seq
                    if waited.get(key.num, 0) < val and need.get(key.num, (None, 0))[1] < val:
                        need[key.num] = (key, val)
                if op.dprev is not None:
                    key, val = op.dprev
                    if waited.get(key.num, 0) < val and need.get(key.num, (None, 0))[1] < val:
                        need[key.num] = (key, val)
                for k, (key, val) in need.items():
                    eng.wait_ge(key, val)
                    waited[k] = val
                    if self.log is not None:
                        self.log.append(f"{ename} WAIT {key.name}>={val}")
                if self.log is not None:
                    self.log.append(f"{ename} OP line{op.tag} seq={op.seq} dma={(op.dsem.name, op.dval) if op.dma else None}")
                ins = op.fn(eng)
                if op.dma:
                    ins.then_inc(op.dsem, 16)
                elif op.seq is not None:
                    ins.then_inc(self.sem[ename], 1)
            last = {}
            for op in ops:
                if op.dma:
                    last[op.dsem.num] = (op.dsem, op.dval)
            for k, (key, val) in last.items():
                if waited.get(k, 0) < val:
                    eng.wait_ge(key, val)

        with nc.Block() as block:
            @block.tensor
            def _(e):
                run("pe", e)

            @block.scalar
            def _(e):
                run("act", e)

            @block.vector
            def _(e):
                run("dve", e)

            @block.gpsimd
            def _(e):
                run("pool", e)

            @block.sync
            def _(e):
                run("sp", e)


def _consts():
    ident = np.eye(128, dtype=np.float32)
    band = np.zeros((128, 12, 128), np.float32)
    invc = np.zeros((4, 16), np.float32)
    t = np.arange(128)[:, None]
    tp = np.arange(128)[None, :]
    for g, w in enumerate(WINDOWS):
        cur = ((tp - t >= 0) & (tp - t < w)).astype(np.float32)
        band[:, g, :] = cur - w * np.eye(128, dtype=np.float32)
        band[:, 4 + g, :] = (t - 128 > tp - w).astype(np.float32)
        cnt = np.minimum(np.arange(128) + 1, w).astype(np.float32)
        band[:, 8 + g, :] = cur - np.diag(cnt)
        invc[g] = 1.0 / cnt[:16]
    tril_t = (tp >= t).astype(np.float32)
    sel = np.zeros((120, 2, 4, NS), np.float32)
    for half in range(2):
        for tokl in range(8):
            for j in range(PBUF):
                for g, w in enumerate(WINDOWS):
                    if j >= PBUF - (w - 1):
                        sel[tokl * PBUF + j, half, g, half * 8 + tokl] = 1.0
    return ident, band.reshape(128, 12 * 128), invc.reshape(1, 64), tril_t, sel.reshape(120, 128)


def build(stop_after=None, dbg=()):
    nc = bass.Bass("TRN2", target_bir_lowering=False)
    S = Sched(nc)

    def din(name, shape, dt=F32):
        return nc.dram_tensor(name, list(shape), dt, kind="ExternalInput").ap()

    def dout(name, shape, dt=F32):
        return nc.dram_tensor(name, list(shape), dt, kind="ExternalOutput").ap()

    xp = din("xp", [T, D])
    xs = din("xs", [NS, D])
    spd = din("sp", [NS, PBUF, DA])
    cs = din("cs", [NS, D])
    cp = din("cp", [1, D])
    w_ada = din("w_ada", [D, 3 * D])
    b_ada = din("b_ada", [1, 3 * D])
    ng_l = din("ng_l", [128, 8])
    w_in = din("w_in", [D, DIN])
    pw_l = din("pw_l", [128, 512])
    pb_l = din("pb_l", [128, 4])
    psc_l = din("psc_l", [128, 4])
    psc_row = din("psc_row", [1, 512])
    lng_row = din("lng_row", [1, 512])
    lnb_row = din("lnb_row", [1, 512])
    swt_l = din("swt_l", [128, 512])
    sgb_row = din("sgb_row", [1, 512])
    w00_row = din("w00_row", [1, 4])
    b0_row = din("b0_row", [1, 4])
    w_ba = din("w_ba", [DA, D])
    w_bb = din("w_bb", [DA, D])
    w_out = din("w_out", [D, D])
    fg_row = din("fg_row", [1, D])
    k_ident = din("k_ident", [128, 128])
    k_band = din("k_band", [128, 12 * 128])
    k_invc = din("k_invc", [1, 64])
    k_tril = din("k_tril", [128, 128])
    k_sel = din("k_sel", [120, 128])

    yp = dout("yp", [T, D])
    ys = dout("ys", [NS, D])
    npp = dout("npp", [16, DA])
    nps = dout("nps", [NS, PBUF, DA])
    nvp = dout("nvp", [128, DA])
    nvs = dout("nvs", [NS, DA])
    dbg_out = {}

    off = [SBUF_BASE]
    sizes = {F32: 4, BF16: 2}

    def nbytes(shape, dt):
        n = sizes[dt]
        for s in shape[1:]:
            n *= s
        return (n + 31) // 32 * 32

    def sb(name, shape, dt, at=None):
        if at is None:
            at = off[0]
            off[0] += nbytes(shape, dt)
            assert off[0] <= SBUF_LIMIT, (name, off[0])
        return nc.alloc_sbuf_tensor_at(name, list(shape), dt, offset=at), at

    def SB(name, shape, dt):
        return sb(name, shape, dt)[0]

    w_in_bf = SB("w_in_bf", [128, 8, DIN], BF16)
    w_ba_bf = SB("w_ba_bf", [128, 4, D], BF16)
    w_bb_bf = SB("w_bb_bf", [128, 4, D], BF16)
    w_out_bf, o_wout = sb("w_out_bf", [128, 8, D], BF16)
    wag = [sb("wag0", [128, 8, 512], BF16, at=o_wout)[0], sb("wag1", [128, 8, 512], BF16, at=o_wout + 8192)[0]]
    ident_f = SB("ident_f", [128, 128], F32)
    ident_b = SB("ident_b", [128, 128], BF16)
    band_bf = SB("band_bf", [128, 12, 128], BF16)
    invc = SB("invc", [128, 4, 16], F32)
    WtT = SB("WtT", [128, 4, 128], BF16)
    Dg = SB("Dg", [NS, 4, NS], BF16)
    pool_w_bf = SB("pool_w_bf", [128, 4, 128], BF16)
    pb2 = SB("pb2", [128, 4], F32)
    pb_t = SB("pb_t", [128, 4], F32)
    psc_t = SB("psc_t", [128, 4], F32)
    ng = SB("ng", [128, 8], F32)
    lng_bc = SB("lng_bc", [128, 512], F32)
    lnb_bc = SB("lnb_bc", [128, 512], F32)
    fg_bc = SB("fg_bc", [128, D], F32)
    gateB = SB("gateB", [128, D], F32)
    gate_s = SB("gate_s", [PW, D], F32)
    brow = SB("brow", [65, 4, 128], BF16)
    ones_bf = SB("ones_bf", [65, 128], BF16)
    ones_f, o_onesf = sb("ones_f", [33, 128], F32)
    negh = SB("negh", [128, 1], F32)
    modT = SB("modT", [128, 16, PW], F32)
    s1all = SB("s1all", [128, 8, PW], F32)
    s1p = SB("s1p", [128, 8], F32)
    b0_bc = SB("b0_bc", [128, 4], F32)
    w00_bc = SB("w00_bc", [NS, 4], F32)
    ss = [SB(f"ss{i}", [128, 1], F32) for i in range(2)]
    msq = [SB(f"msq{i}", [128, 1], F32) for i in range(2)]
    rr = [SB(f"rr{i}", [128, 1], F32) for i in range(2)]
    st6 = [SB(f"st6{i}", [128, 6], F32) for i in range(2)]
    mv = [SB(f"mv{i}", [128, 2], F32) for i in range(2)]
    vtmp = [SB(f"vtmp{i}", [128, 1], F32) for i in range(2)]
    rstd = [SB(f"rstd{i}", [128, 1], F32) for i in range(2)]
    ss2 = [SB(f"ss2{i}", [128, 1], F32) for i in range(2)]
    ms2 = [SB(f"ms2{i}", [128, 1], F32) for i in range(2)]
    r2 = [SB(f"r2{i}", [128, 1], F32) for i in range(2)]
    hT, o_hT = sb("hT", [128, 8, 512], BF16)
    a_bf, o_abf = sb("a_bf", [128, 4, 512], BF16)
    a_prev, o_aprev = sb("a_prev", [128, 512], BF16)
    vn, o_vn = sb("vn", [128, 4, 512], BF16)
    ug, o_ug = sb("ug", [128, 4, 512], BF16)
    sga, o_sga = sb("sga", [128, 4, 512], BF16)
    sgb_o = off[0]
    sgb = [SB(f"sgb{i}", [128, 512], BF16) for i in range(2)]
    pooledT = [SB(f"pooledT{i}", [128, 512], BF16) for i in range(2)]
    mergedT, o_mg = sb("mergedT", [128, 8, 512], BF16)
    xb_o = []
    xb = []
    for i in range(2):
        t_, o_ = sb(f"xb{i}", [128, D], F32)
        xb.append(t_)
        xb_o.append(o_)
    xr_o = off[0]
    xr = [SB(f"xr{i}", [128, D], F32) for i in range(2)]
    xn_o = off[0]
    xn = [SB(f"xn{i}", [128, D], BF16) for i in range(2)]
    jk_o = off[0]
    jk = SB("jk", [128, D], BF16)
    xn2 = [jk, jk]
    vg_o = []
    vg = []
    for i in range(2):
        t_, o_ = sb(f"vg{i}", [128, 512], F32)
        vg.append(t_)
        vg_o.append(o_)
    tt_o = []
    tt = []
    for i in range(2):
        t_, o_ = sb(f"tt{i}", [128, 512], F32)
        tt.append(t_)
        tt_o.append(o_)
    sma_o = off[0]
    sma = [SB(f"sma{i}", [128, 512], BF16) for i in range(2)]
    alast = None
    smb = [SB(f"smb{i}", [128, 512], BF16) for i in range(2)]
    t1_o = []
    t1 = []
    for i in range(2):
        t_, o_ = sb(f"t1{i}", [128, 512], BF16)
        t1.append(t_)
        t1_o.append(o_)
    t2 = [SB(f"t2{i}", [128, 512], BF16) for i in range(2)]
    cT_o = off[0]
    hT_s = SB("hT_s", [128, 8, NS], BF16)
    ug_s = SB("ug_s", [128, 4, NS], BF16)
    sga_s = SB("sga_s", [128, 4, NS], BF16)
    mergedT_s = SB("mergedT_s", [128, 8, NS], BF16)
    print("SBUF bytes/partition used:", off[0])

    xr_o0 = None
    wa = [sb("wa0", [128, 8, 512], BF16, at=o_mg)[0], sb("wa1", [128, 8, 512], BF16, at=o_hT)[0]]
    wa.append(sb("wa2", [128, 8, 512], BF16, at=xr_o)[0])
    wa.append(sb("wa3", [128, 8, 512], BF16, at=sma_o)[0])
    mod_sb = sb("mod_sb", [PW, 3 * D], F32, at=o_vn)[0]
    assert o_ug == o_vn + 4096 and o_sga == o_ug + 4096
    cin = sb("cin", [PW, D], F32, at=o_abf)[0]
    ba = [sb("ba0", [PW, 512], F32, at=vg_o[0])[0], sb("ba1", [PW, 512], F32, at=vg_o[1])[0],
          sb("ba2", [PW, 512], F32, at=tt_o[0])[0], sb("ba3", [PW, 512], F32, at=tt_o[1])[0]]
    stg_pw = sb("stg_pw", [128, 512], F32, at=sgb_o)[0]
    stg_psc = sb("stg_psc", [128, 512], F32, at=sgb_o + 2048)[0]
    stg_sw = sb("stg_sw", [128, 512], F32, at=jk_o)[0]
    stg_tril = sb("stg_tril", [128, 128], F32, at=o_aprev)[0]
    browf = sb("browf", [65, 512], F32, at=xn_o)[0]
    browh = sb("browh", [65, 512], BF16, at=xn_o + 2048)[0]
    pfx = sb("pfx", [120, 2, 512], F32, at=xr_o)[0]
    a_s = sb("a_s", [NS, 512], F32, at=xr_o + 4096)[0]
    vn_s = sb("vn_s", [NS, 1, 512], BF16, at=xr_o + 6144)[0]
    sel_sb = sb("sel_sb", [120, 128], F32, at=o_onesf)[0]
    psumT = invc
    tmpS = sb("tmpS", [128, 8, NS], F32, at=tt_o[1])[0]

    banks = [nc.alloc_psum_tensor(f"pbk{i}", [128, 512], F32) for i in range(6)]
    psT = [nc.alloc_psum_tensor(f"ptr{i}", [128, 1024], BF16) for i in range(2)]
    Bbank = [Buf(f"bank{i}") for i in range(6)]
    BpsT = [Buf(f"psT{i}") for i in range(2)]
    bank_ctr = [0]

    def nextbank():
        i = bank_ctr[0] % 6
        bank_ctr[0] += 1
        return banks[i], Bbank[i]

    B = {}

    def bf(name):
        if name not in B:
            B[name] = Buf(name)
        return B[name]

    Bw_in = [bf(f"w_in{g}") for g in range(9)]
    B_abf, B_aprev, B_vn, B_ug, B_sga, B_mg = (bf(n) for n in ("a_bf", "a_prev", "vn", "ug", "sga", "mergedT"))
    B_hT = [bf(f"hT{i}") for i in range(4)]
    B_xb = [bf(f"xb{i}") for i in range(2)]
    B_xr = [bf(f"xr{i}") for i in range(2)]
    B_xn = [bf(f"xn{i}") for i in range(2)]
    B_vg = [bf(f"vg{i}") for i in range(2)]
    B_tt = [bf(f"tt{i}") for i in range(2)]
    B_t1 = [bf(f"t1{i}") for i in range(2)]
    B_t2 = [bf(f"t2{i}") for i in range(2)]
    B_sma = [bf(f"sma{i}") for i in range(2)]
    B_smb = [bf(f"smb{i}") for i in range(2)]
    B_sgb = [bf(f"sgb{i}") for i in range(2)]
    B_pT = [bf(f"pooledT{i}") for i in range(2)]
    B_mod = [B_vn, B_ug, B_sga]
    xr.append(sb("xr2", [128, D], F32, at=o_ug)[0])
    xr.append(sb("xr3", [128, D], F32, at=o_sga)[0])
    B_xr.extend([B_ug, B_sga])
    alast = sb("alast", [128, 512], F32, at=tt_o[0])[0]
    B_alast = [B_tt[0]]
    B_xn2 = [bf("jk"), bf("jk")]

    def dma(q, out, in_, r, w):
        return S.add(q, lambda e, o=out, i=in_: e.dma_start(out=o, in_=i), r, w, dma=True)

    def mm(out, lhsT, rhs, start, stop, r, w):
        return S.add("pe", lambda e, o=out, l=lhsT, rh=rhs, s0=start, s1=stop: e.matmul(o, l, rh, start=s0, stop=s1), r, w)

    def tr(out, in_, ident, r, w):
        return S.add("pe", lambda e, o=out, i=in_, idn=ident: e.transpose(o, i, idn), r, w)

    def act(out, in_, func, r, w, **kw):
        return S.add("act", lambda e, o=out, i=in_, f=func, k=kw: e.activation(o, i, f, **k), r, w)

    def ts(eng, out, in0, s1, s2, op0, op1, r, w, indep=False):
        if s2 is None:
            return S.add(eng, lambda e, o=out, i=in0, a=s1, p0=op0: e.tensor_scalar(o, i, a, None, p0), r, w, indep=indep)
        return S.add(eng, lambda e, o=out, i=in0, a=s1, b=s2, p0=op0, p1=op1: e.tensor_scalar(o, i, a, b, p0, p1), r, w,
                     indep=indep)

    def tten(eng, out, in0, in1, op, r, w):
        return S.add(eng, lambda e, o=out, a=in0, b=in1, p=op: e.tensor_tensor(o, a, b, p), r, w)

    def stt(out, in0, scalar, in1, op0, op1, r, w):
        return S.add("dve", lambda e, o=out, a=in0, s=scalar, b=in1, p0=op0, p1=op1: e.scalar_tensor_tensor(o, a, s, b, p0, p1), r, w)

    def cpy(eng, out, in_, r, w):
        return S.add(eng, lambda e, o=out, i=in_: e.tensor_copy(o, i), r, w)

    def mset(eng, ap, val, w):
        return S.add(eng, lambda e, a=ap, v=val: e.memset(a, v), (), w)

    def dump(name, ap, shape, bufs, dt=BF16):
        if name not in dbg:
            return
        d = dout("dbg_" + name, shape, dt)
        dbg_out[name] = shape
        dma("sp", d, ap, bufs, [bf("dbg_" + name)])

    w_ada_v = w_ada.rearrange("(k p) n -> p k n", p=128)
    w_in_v = w_in.rearrange("(k p) n -> p k n", p=128)
    B_wa = [[B_mg], B_hT, [B_xr[0], B_xr[1]], B_sma + B_smb + B_t1 + B_t2]
    B_ba = [B_vg[0], B_vg[1], B_tt[0], B_tt[1]]
    B_spw, B_spsc = [B_sgb[0], B_sgb[1]], [B_pT[0], B_pT[1]]
    NWA = 4
    cT = sb("cT", [128, 8, PW], BF16, at=cT_o)[0]
    B_cT = [bf("hT_s"), bf("ug_s"), bf("sga_s"), bf("mergedT_s")]

    chain = []
    DEPTH = 8

    def cdma(out, in_, w, extra_r=(), depth=None):
        depth = DEPTH if depth is None else depth
        r = list(extra_r)
        if len(chain) >= depth:
            r.append(chain[-depth])
        t = bf(f"chain{len(chain)}")
        chain.append(t)
        dma("pool", out, in_, r, list(w) + [t])

    def wdma(g, extra_r=()):
        cdma(w_in_bf[:, :, g * 512:(g + 1) * 512], w_in_v[:, :, g * 512:(g + 1) * 512], [Bw_in[g]], extra_r)

    wq = []

    def wq_pop(n):
        for _ in range(n):
            if wq:
                wq.pop(0)()

    def setup_a():
        dma("sp", ident_f[:, :], k_ident, [], [bf("ident_f")])
        mset("pool", negh[:, :], -0.5, [bf("negh")])
        mset("pool", cin[:, :], 0.0, [B_abf])
        dma("sp", cin[0:NS, :], cs, [], [B_abf])
        dma("sp", cin[32:33, :], cp, [], [B_abf])
        blocks[0]["xload"](0)
        blocks[0]["xload"](1)
        mset("dve", browf[:, :], 0.0, [B_xn[0]])
        dma("sp", browf[0:1, :], sgb_row, [], [B_xn[0]])
        dma("sp", browf[32:33, :], sgb_row, [], [B_xn[0]])
        dma("sp", browf[64:65, :], sgb_row, [], [B_xn[0]])
        dma("sp", ng[:, :], ng_l, [], [bf("ng")])
        for j in range(4):
            dma("sp", ba[j][:, :], b_ada[0:1, j * 512:(j + 1) * 512].partition_broadcast(PW), [], [B_ba[j]])
        for j in range(2):
            cdma(wa[j][:, :, :], w_ada_v[:, :, j * 512:(j + 1) * 512], B_wa[j])
        dma("pool", ident_b[:, :], k_ident, [], [bf("ident_b")])
        dma("pool", band_bf[:, :, :], k_band.rearrange("p (i t) -> p i t", t=128), [], [bf("band")])
        for j in range(2, NWA):
            cdma(wa[j][:, :, :], w_ada_v[:, :, j * 512:(j + 1) * 512], B_wa[j], depth=2)
        dma("sp", invc[:, :, :], k_invc.rearrange("o (g t) -> o g t", t=16).partition_broadcast(128), [], [bf("invc")])
        dma("sp", stg_pw[:, :], pw_l, [], B_spw)
        dma("sp", stg_psc[:, :], psc_row.partition_broadcast(128), [], B_spsc)
        dma("sp", pb_t[:, :], pb_l, [], [bf("pb_t")])
        dma("sp", psc_t[:, :], psc_l, [], [bf("psc_t")])
        dma("sp", lng_bc[:, :], lng_row.partition_broadcast(128), [], [bf("lng")])
        dma("sp", lnb_bc[:, :], lnb_row.partition_broadcast(128), [], [bf("lnb")])
        dma("sp", fg_bc[:, :], fg_row.partition_broadcast(128), [], [bf("fg")])
        dma("sp", b0_bc[:, :], b0_row.partition_broadcast(128), [], [bf("b0")])
        dma("sp", w00_bc[:, :], w00_row.partition_broadcast(NS), [], [bf("w00")])
        dma("sp", stg_sw[:, :], swt_l, [], [bf("jk")])
        dma("sp", stg_tril[:, :], k_tril, [], [B_aprev])

        mset("dve", ones_bf[:, :], 1.0, [bf("ones_bf")])
        mset("dve", ones_f[:, :], 1.0, [bf("ones_f")])
        browv = brow[:, :, :].rearrange("p g t -> p (g t)")
        cpy("dve", browh[:, :], browf[:, :], [B_xn[0]], [B_xn[1]])
        cpy("dve", browv[0:32, :], browh[0:32, :], [B_xn[1]], [bf("brow")])
        tten("dve", browf[32:64, :], browf[32:64, :], browh[32:64, :], ALU.subtract,
             [B_xn[0], B_xn[1]], [B_xn[0]])
        tten("dve", browf[64:65, :], browf[64:65, :], browh[64:65, :], ALU.subtract,
             [B_xn[0], B_xn[1]], [B_xn[0]])
        cpy("dve", browv[32:64, :], browf[32:64, :], [B_xn[0]], [bf("brow")])
        cpy("dve", browh[64:65, :], browf[64:65, :], [B_xn[0]], [B_xn[1]])
        tten("dve", browv[64:65, :], browf[64:65, :], browh[64:65, :], ALU.subtract,
             [B_xn[0], B_xn[1]], [bf("brow")])

    def small_prep():
        tten("dve", pool_w_bf[:, :, :].rearrange("p g d -> p (g d)"), stg_pw[:, :], stg_psc[:, :], ALU.mult,
             B_spw + B_spsc, [bf("pool_w")])
        tten("dve", pb2[:, :], pb_t[:, :], psc_t[:, :], ALU.mult, [bf("pb_t"), bf("psc_t")], [bf("pb2")])
        for g in range(4):
            tten("dve", WtT[:, g, :], stg_sw[:, g * 128:(g + 1) * 128], stg_tril[:, :], ALU.mult,
                 [bf("jk"), B_aprev], [bf("WtT")])
            ts("dve", Dg[:, g, :], ident_f[0:NS, 0:NS], w00_bc[:, g:g + 1], None, ALU.mult, None,
               [bf("ident_f"), bf("w00")], [bf("Dg")])

    def build_wq():
        for g in (0, 3):
            wdma(g)

    def build_wq2():
        for g in (2, 1, 4):
            wdma(g)
        for j in (4, 5):
            cdma(wag[j - 4][:, :, :], w_ada_v[:, :, j * 512:(j + 1) * 512], [bf(f"w_out_g{j - 4}")])
        for g in (5, 7):
            wdma(g)
        cdma(w_ba_bf[:, :, :], w_ba.rearrange("(k p) n -> p k n", p=128), [bf("w_ba")])
        cdma(w_bb_bf[:, :, :], w_bb.rearrange("(k p) n -> p k n", p=128), [bf("w_bb")])
        for g in (6, 8):
            wdma(g)
        wq.append(lambda: [dma("sp", ba[j - 2][:, :], b_ada[0:1, j * 512:(j + 1) * 512].partition_broadcast(PW), [],
                               [B_ba[j - 2]]) for j in (4, 5)])

    def w_out_dma():
        cdma(w_out_bf[:, :, :], w_out.rearrange("(k p) n -> p k n", p=128), [bf("w_out"), bf("w_out_g0"), bf("w_out_g1")])

    def mod_compute(j, wi, bi, dest, dtags, wsrc=None, wtags=None):
        wsrc = wa[wi] if wsrc is None else wsrc
        wtags = B_wa[wi] if wtags is None else wtags
        bk, Bk = nextbank()
        for k in range(8):
            mm(bk[0:PW, :], cT[:, k, :], wsrc[:, k, :], k == 0, k == 7, B_cT + wtags, [Bk])
        tten("dve", dest, bk[0:PW, :], ba[bi][:, :], ALU.add, [Bk, B_ba[bi]], dtags)

    def gate_part():
        for j in (4, 5):
            mod_compute(j, j - 2, j - 2, gate_s[:, (j - 4) * 512:(j - 3) * 512], [bf("gate_s")],
                        wsrc=wag[j - 4], wtags=[bf(f"w_out_g{j - 4}")])
        for hh in range(2):
            bk, Bk = nextbank()
            mm(bk[:, :], ones_f[32:33, :], gate_s[32:33, hh * 512:(hh + 1) * 512], True, True,
               [bf("gate_s"), bf("ones_f")], [Bk])
            cpy("dve", gateB[:, hh * 512:(hh + 1) * 512], bk[:, :], [Bk], [bf("gateB")])

    def setup_b():
        act(cin[:, :], cin[:, :], AF.Silu, [B_abf], [B_abf])
        bk, Bk = nextbank()
        for k in range(8):
            tr(bk[:, k * PW:(k + 1) * PW], cin[0:PW, k * 128:(k + 1) * 128], ident_f[0:PW, 0:PW],
               [B_abf, bf("ident_f")], [Bk])
        cpy("dve", cT[:, :, :], bk[:, 0:8 * PW].rearrange("p (k c) -> p k c", c=PW), [Bk], B_cT)

    def setup_b2():
        for hh in range(2):
            for j in (2 * hh, 2 * hh + 1):
                mod_compute(j, j, j, mod_sb[:, j * 512:(j + 1) * 512], B_mod)
            bk, Bk = nextbank()
            for c in range(8):
                cc = hh * 8 + c
                tr(bk[:, c * PW:(c + 1) * PW], mod_sb[0:PW, cc * 128:(cc + 1) * 128], ident_f[0:PW, 0:PW],
                   B_mod + [bf("ident_f")], [Bk])
            cpy("dve", modT[:, hh * 8:(hh + 1) * 8, :], bk[:, 0:8 * PW].rearrange("p (k c) -> p k c", c=PW),
                [Bk], [bf("modT")])
        stt(s1p[:, :], modT[:, 8:16, 32], 1.0, ng[:, :], ALU.add, ALU.mult, [bf("modT"), bf("ng")], [bf("s1p")])

    def s1_sample():
        for k in range(8):
            ts("dve", s1all[:, k, :], modT[:, 8 + k, :], 1.0, ng[:, k:k + 1], ALU.add, ALU.mult,
               [bf("modT"), bf("ng")], [bf("s1all")])

    cnt = {"x": 0, "v": 0, "m": 0, "o": 0, "g": 0}

    def rmsnorm_stats(src, P, Bsrc, i2, junk, Bjunk, ssl, msl, rl, tag):
        act(junk, src, AF.Square, [Bsrc], [Bjunk, bf(f"{tag}ss{i2}")], accum_out=ssl[i2][0:P, :])
        ts("pool", msl[i2][0:P, :], ssl[i2][0:P, :], 1.0 / D, EPS, ALU.mult, ALU.add,
           [bf(f"{tag}ss{i2}")], [bf(f"{tag}ms{i2}")])
        tten("pool", rl[i2][0:P, :], msl[i2][0:P, :], negh[0:P, :], ALU.pow,
             [bf(f"{tag}ms{i2}"), bf("negh")], [bf(f"{tag}r{i2}")])

    def zchunk_fm(hsrc, Bh, col0, NT):
        bk, Bk = nextbank()
        g = col0 // 512
        for k in range(8):
            mm(bk[:, 0:NT], w_in_bf[:, k, col0:col0 + 128], hsrc[:, k, 0:NT], k == 0, k == 7,
               [Bw_in[g]] + list(dict.fromkeys(Bh)), [Bk])
        return bk, Bk

    def ztile_tm(hsrc, Bh, col0, t0, P):
        bk, Bk = nextbank()
        g = col0 // 512
        for k in range(8):
            mm(bk[0:P, :], hsrc[:, k, t0:t0 + P], w_in_bf[:, k, col0:col0 + 512], k == 0, k == 7,
               [Bw_in[g], Bh[t0 // 128]], [Bk])
        return bk, Bk

    def mk_block(b, sample):
        rb = b * 512
        NT = NS if sample else 512
        tiles = [(0, NS)] if sample else [(i * 128, 128) for i in range(4)]
        first = (b == 0) and not sample
        last = (b == 3) and not sample
        if sample:
            hT_, B_hT_ = hT_s, [bf("hT_s")] * 4
            ug_, B_ug_ = ug_s, bf("ug_s")
            sga_, B_sga_ = sga_s, bf("sga_s")
            mg_, B_mg_ = mergedT_s, bf("mergedT_s")
        else:
            hT_, B_hT_, ug_, B_ug_, sga_, B_sga_, mg_, B_mg_ = hT, B_hT, ug, B_ug, sga, B_sga, mergedT, B_mg
        xrl, B_xrl = xr, B_xr
        if sample:
            Ba_, Bap_, vn_, B_vn_ = B_xr[1], B_xr[1], vn_s, B_xr[1]
        else:
            Ba_, Bap_, vn_, B_vn_ = B_abf, B_aprev, vn, B_vn
        xsel = {}

        def xload(i):
            t0, P = tiles[i]
            xi = cnt["x"] % 2
            cnt["x"] += 1
            xsel[i] = xi
            if sample:
                dma("sp", xb[xi][0:P, :], xs, [], [B_xb[xi]])
            else:
                q = "act" if (b == 0 and i >= 2) else "sp"
                dma(q, xb[xi][:, :], xp[rb + t0:rb + t0 + P, :], [], [B_xb[xi]])

        def prenorm(i):
            t0, P = tiles[i]
            if i not in xsel:
                xload(i)
            xi = xsel[i]
            xt, Bxt = xb[xi], B_xb[xi]
            rmsnorm_stats(xt[0:P, :], P, Bxt, xi, xn[xi][0:P, :], B_xn[xi], ss, msq, rr, "n")
            ts("dve", xn[xi][0:P, :], xt[0:P, :], rr[xi][0:P, 0:1], None, ALU.mult, None,
               [Bxt, bf(f"nr{xi}")], [B_xn[xi]])

        def xposeT(i):
            t0, P = tiles[i]
            xi = xsel[i]
            for k in range(8):
                tr(psT[xi][:, k * P:(k + 1) * P], xn[xi][0:P, k * 128:(k + 1) * 128], ident_b[0:P, 0:P],
                   [B_xn[xi], bf("ident_b")], [BpsT[xi]])

        def xposeE(i):
            t0, P = tiles[i]
            xi = xsel[i]
            if sample:
                pv = psT[xi][:, 0:8 * P].rearrange("p (k t) -> p k t", t=P)
                tten("dve", tmpS[:, :, :], pv, s1all[:, :, 0:NS], ALU.mult, [BpsT[xi], bf("s1all")], [B_tt[1]])
                tten("dve", hT_[:, :, 0:NS], tmpS[:, :, :], modT[:, 0:8, 0:NS], ALU.add,
                     [B_tt[1], bf("modT")], [B_hT_[0]])
            else:
                for k in range(8):
                    ts("dve", hT_[:, k, t0:t0 + P], psT[xi][:, k * P:(k + 1) * P], s1p[:, k:k + 1],
                       modT[:, k, 32:33], ALU.mult, ALU.add, [BpsT[xi], bf("s1p"), bf("modT")], [B_hT_[i]],
                       indep=(k >= 1))

        def xpose(i):
            xposeT(i)
            xposeE(i)

        def prefix():
            dma("sp", nps[:, 0:14, :], spd[:, 1:15, :], [], [bf("o_nps0")])
            spd2 = spd.rearrange("t j c -> (t j) c")
            for h in range(2):
                dma("sp", pfx[:, h, :], spd2[h * 120:(h + 1) * 120, :], [], [B_xr[0]])
            dma("sp", sel_sb[:, :], k_sel, [], [bf("ones_f")])

        def body(pre_merge=None):
            for _ in _body(pre_merge):
                pass

        def _body(pre_merge):
          if True:
            if b == 0 and not sample:
                dump("hT", hT_[:, :, :], [128, 8, 512], B_hT_)
            def a_tile(i, t0, P):
                bk, Bk = ztile_tm(hT_, B_hT_, C_A, t0, P)
                if sample:
                    cpy("dve", a_s[:, :], bk[0:P, :], [Bk], [Ba_])
                    dma("sp", nps[:, 14, :], a_s[:, :], [Ba_], [bf("o_nps1")])
                else:
                    act(a_bf[:, i, :], bk[:, :], AF.Copy, [Bk], [Ba_])
                    if last and i == 3:
                        bk2, Bk2 = ztile_tm(hT_, B_hT_, C_A, t0, P)
                        cpy("dve", alast[:, :], bk2[:, :], [Bk2], B_alast)
                        dma("sp", npp, alast[112:128, :], B_alast, [bf("o_npp")])
            def ln_tile(i, t0, P, f32_out):
                vi = cnt["v"] % 2
                cnt["v"] += 1
                bk, Bk = ztile_tm(hT_, B_hT_, C_V, t0, P)
                act(vg[vi][0:P, :], bk[0:P, :], AF.Gelu, [Bk], [B_vg[vi]])
                S.add("dve", lambda e, o=st6[vi][0:P, :], a=vg[vi][0:P, :]: e.bn_stats(o, a), [B_vg[vi]], [bf(f"st6{vi}")])
                S.add("dve", lambda e, o=mv[vi][0:P, :], a=st6[vi][0:P, :]: e.bn_aggr(o, a), [bf(f"st6{vi}")], [bf(f"mv{vi}")])
                ts("pool", vtmp[vi][0:P, :], mv[vi][0:P, 1:2], EPS, 1.0, ALU.add, ALU.mult, [bf(f"mv{vi}")], [bf(f"vtmp{vi}")])
                tten("pool", rstd[vi][0:P, :], vtmp[vi][0:P, :], negh[0:P, :], ALU.pow,
                     [bf(f"vtmp{vi}"), bf("negh")], [bf(f"rstd{vi}")])
                stt(vg[vi][0:P, :], vg[vi][0:P, :], mv[vi][0:P, 0:1], lng_bc[0:P, :], ALU.subtract, ALU.mult,
                    [B_vg[vi], bf(f"mv{vi}"), bf("lng")], [B_vg[vi]])
                if f32_out:
                    stt(vg[vi][0:P, :], vg[vi][0:P, :], rstd[vi][0:P, 0:1], lnb_bc[0:P, :], ALU.mult, ALU.add,
                        [B_vg[vi], bf(f"rstd{vi}"), bf("lnb")], [B_vg[vi]])
                    dma("sp", nvs if sample else nvp, vg[vi][0:P, :], [B_vg[vi]], [bf("o_nv" + ("s" if sample else "p"))])
                else:
                    stt(vn_[0:P, i, :], vg[vi][0:P, :], rstd[vi][0:P, 0:1], lnb_bc[0:P, :], ALU.mult, ALU.add,
                        [B_vg[vi], bf(f"rstd{vi}"), bf("lnb")], [B_vn_])

            for i, (t0, P) in enumerate(tiles):
                if i in (1, 3):
                    wq_pop(2)
                a_tile(i, t0, P)
                ln_tile(i, t0, P, False)
                if sample or (last and i == 3):
                    ln_tile(i, t0, P, True)
            if b == 0 and not sample:
                dump("vn", vn_[:, :, :], [128, 4, 512], [B_vn_])
                dump("a_bf", a_bf[:, :, :], [128, 4, 512], [Ba_])
            yield
            for j in range(4):
                wq_pop(1)
                bk, Bk = zchunk_fm(hT_, B_hT_, C_U + j * 128, NT)
                act(ug_[:, j, 0:NT], bk[:, 0:NT], AF.Gelu, [Bk], [B_ug_])
            if sample:
                bkp, Bkp = nextbank()
                for g, w in enumerate(WINDOWS):
                    for h in range(2):
                        mm(bkp[:, g * NS:(g + 1) * NS], pfx[:, h, g * 128:(g + 1) * 128],
                           sel_sb[:, (h * 4 + g) * NS:(h * 4 + g + 1) * NS], h == 0, h == 1,
                           [B_xr[0], bf("ones_f")], [Bkp])
                for g, w in enumerate(WINDOWS):
                    ts("dve", psumT[:, g, :], bkp[:, g * NS:(g + 1) * NS], 1.0 / w, None, ALU.mult, None,
                       [Bkp], [bf("invc")])
            for g, w in enumerate(WINDOWS):
                pi = cnt["g"] % 2
                cnt["g"] += 1
                bk, Bk = nextbank()
                if sample:
                    bka_, Bka_ = zchunk_fm(hT_, B_hT_, C_A + g * 128, NT)
                    stt(pooledT[pi][:, 0:NS], bka_[:, 0:NS], 1.0 / w - 1.0, psumT[:, g, :], ALU.mult, ALU.add,
                        [Bka_, bf("invc")], [B_pT[pi]])
                else:
                    for i in range(4):
                        seq_first = first and i == 0
                        o = bk[:, i * 128:(i + 1) * 128]
                        mm(o, a_bf[:, i, g * 128:(g + 1) * 128], band_bf[:, (8 + g) if seq_first else g, :], True, seq_first,
                           [Ba_, bf("band")], [Bk])
                        if not seq_first:
                            if i == 0:
                                mm(o, a_prev[:, g * 128:(g + 1) * 128], band_bf[:, 4 + g, :], False, True,
                                   [Bap_, bf("band")], [Bk])
                            else:
                                mm(o, a_bf[:, i - 1, g * 128:(g + 1) * 128], band_bf[:, 4 + g, :], False, True,
                                   [Ba_, bf("band")], [Bk])
                    act(pooledT[pi][:, :], bk[:, :], AF.Copy, [Bk], [B_pT[pi]], scale=1.0 / w)
                bk2, Bk2 = zchunk_fm(hT_, B_hT_, C_GA + g * 128, NT)
                act(sga_[:, g, 0:NT], bk2[:, 0:NT], AF.Silu, [Bk2], [B_sga_])
                bk3, Bk3 = nextbank()
                mm(bk3[:, 0:NT], pool_w_bf[:, g, :], pooledT[pi][:, 0:NT], True, True, [bf("pool_w"), B_pT[pi]], [Bk3])
                if first:
                    stt(bk3[:, 0:16], bk3[:, 0:16], float(w), invc[:, g, :], ALU.mult, ALU.mult,
                        [Bk3, bf("invc")], [Bk3])
                stt(sga_[:, g, 0:NT], bk3[:, 0:NT], pb2[:, g:g + 1], sga_[:, g, 0:NT], ALU.add, ALU.mult,
                    [Bk3, bf("pb2"), B_sga_], [B_sga_])
            if not sample:
                cpy("dve", a_prev[:, :], a_bf[:, 3, :], [Ba_], [Bap_])
            if b == 0 and not sample:
                dump("yaT", sga_[:, :, :], [128, 4, 512], [B_sga_])
            yield
            for g in range(4):
                si = cnt["g"] % 2
                cnt["g"] += 1
                wq_pop(1)
                bk, Bk = zchunk_fm(hT_, B_hT_, C_GB + g * 128, NT)
                act(sgb[si][:, 0:NT], bk[:, 0:NT], AF.Silu, [Bk], [B_sgb[si]])
                tten("pool", ug_[:, g, 0:NT], ug_[:, g, 0:NT], sgb[si][:, 0:NT], ALU.mult, [B_ug_, B_sgb[si]], [B_ug_])
                bk2, Bk2 = nextbank()
                if sample:
                    mm(bk2[:, 0:NS], vn_[0:NS, 0, g * 128:(g + 1) * 128], Dg[:, g, :], True, True, [B_vn_, bf("Dg")], [Bk2])
                    stt(ug_[:, g, 0:NS], bk2[:, 0:NS], b0_bc[:, g:g + 1], ug_[:, g, 0:NS], ALU.add, ALU.mult,
                        [Bk2, bf("b0"), B_ug_], [B_ug_])
                else:
                    mm(bk2[:, :], ones_bf[0:65, :], brow[0:65, g:g + 1, :].to_broadcast([65, 4, 128]), True, False,
                       [bf("ones_bf"), bf("brow")], [Bk2])
                    for i in range(4):
                        o = bk2[:, i * 128:(i + 1) * 128]
                        mm(o, vn_[:, i, g * 128:(g + 1) * 128], WtT[:, g, :], False, i == 3, [B_vn_, bf("WtT")], [Bk2])
                    tten("dve", ug_[:, g, :], bk2[:, :], ug_[:, g, :], ALU.mult, [Bk2, B_ug_], [B_ug_])
            if b == 0 and not sample:
                dump("ybT", ug_[:, :, :], [128, 4, 512], [B_ug_])
            yield
            if pre_merge is not None:
                pre_merge()
            wq_pop(8)
            for i in range(8):
                mi = cnt["m"] % 2
                cnt["m"] += 1
                bk, Bk = zchunk_fm(hT_, B_hT_, C_MA + i * 128, NT)
                act(sma[mi][:, 0:NT], bk[:, 0:NT], AF.Sigmoid, [Bk], [B_sma[mi]])
                bk, Bk = zchunk_fm(hT_, B_hT_, C_MB + i * 128, NT)
                act(smb[mi][:, 0:NT], bk[:, 0:NT], AF.Sigmoid, [Bk], [B_smb[mi]])
                bka, Bka = nextbank()
                for c in range(4):
                    mm(bka[:, 0:NT], w_ba_bf[:, c, i * 128:(i + 1) * 128], sga_[:, c, 0:NT], c == 0, c == 3,
                       [bf("w_ba"), B_sga_], [Bka])
                bkb, Bkb = nextbank()
                for c in range(4):
                    mm(bkb[:, 0:NT], w_bb_bf[:, c, i * 128:(i + 1) * 128], ug_[:, c, 0:NT], c == 0, c == 3,
                       [bf("w_bb"), B_ug_], [Bkb])
                tten("dve", t1[mi][:, 0:NT], bka[:, 0:NT], sma[mi][:, 0:NT], ALU.mult, [Bka, B_sma[mi]], [B_t1[mi]])
                tten("dve", t2[mi][:, 0:NT], bkb[:, 0:NT], smb[mi][:, 0:NT], ALU.mult, [Bkb, B_smb[mi]], [B_t2[mi]])
                tten("pool", mg_[:, i, 0:NT], t1[mi][:, 0:NT], t2[mi][:, 0:NT], ALU.add, [B_t1[mi], B_t2[mi]], [B_mg_])
            if b == 0 and not sample:
                dump("mergedT", mg_[:, :, :], [128, 8, 512], [B_mg_])
        osel = {}

        def rload(i):
            t0, P = tiles[i]
            if sample:
                dma("sp", xb[0][0:P, :], xs, [], [B_xb[0]])
                return
            dma("sp", xrl[i][:, :], xp[rb + t0:rb + t0 + P, :], [], [B_xrl[i]])

        def tailA(i):
            t0, P = tiles[i]
            oi = cnt["o"] % 2
            cnt["o"] += 1
            osel[i] = oi
            if sample:
                xt, Bxt = xb[0], B_xb[0]
            else:
                xt, Bxt = xrl[i], B_xrl[i]
            for hh in range(2):
                bk, Bk = nextbank()
                for k in range(8):
                    mm(bk[0:P, :], mg_[:, k, t0:t0 + P], w_out_bf[:, k, hh * 512:(hh + 1) * 512], k == 0, k == 7,
                       [B_mg_, bf("w_out")], [Bk])
                gsrc = gate_s[0:P, hh * 512:(hh + 1) * 512] if sample else gateB[:, hh * 512:(hh + 1) * 512]
                tten("dve", tt[hh][0:P, :], bk[0:P, :], gsrc, ALU.mult, [Bk, bf("gate_s" if sample else "gateB")], [B_tt[hh]])
                tten("dve", xt[0:P, hh * 512:(hh + 1) * 512], xt[0:P, hh * 512:(hh + 1) * 512], tt[hh][0:P, :], ALU.add,
                     [Bxt, B_tt[hh]], [Bxt])
            rmsnorm_stats(xt[0:P, :], P, Bxt, oi, xn2[oi][0:P, :], B_xn2[oi], ss2, ms2, r2, "f")

        def tailB(i):
            t0, P = tiles[i]
            oi = osel[i]
            if sample:
                xt, Bxt = xb[0], B_xb[0]
            else:
                xt, Bxt = xrl[i], B_xrl[i]
            stt(xt[0:P, :], xt[0:P, :], r2[oi][0:P, 0:1], fg_bc[0:P, :], ALU.mult, ALU.mult,
                [Bxt, bf(f"fr{oi}"), bf("fg")], [Bxt])
            if sample:
                dma("sp", ys, xt[0:P, :], [Bxt], [bf("o_ys")])
            else:
                dma("sp", yp[rb + t0:rb + t0 + P, :], xt[:, :], [Bxt], [bf(f"o_yp{i}")])

        return {"prenorm": prenorm, "xload": xload, "rload": rload, "xpose": xpose, "xposeT": xposeT,
                "xposeE": xposeE, "body": body,
                "bodygen": _body, "prefix": prefix,
                "tailA": tailA, "tailB": tailB, "n": len(tiles)}

    blocks = [mk_block(b, False) for b in range(4)] + [mk_block(0, True)]
    b0 = blocks[0]
    setup_a()
    build_wq()
    setup_b()
    b0["prenorm"](0)
    b0["prenorm"](1)
    b0["xload"](2)
    b0["xload"](3)
    b0["xposeT"](0)
    b0["xposeT"](1)
    build_wq2()
    setup_b2()
    b0["xposeE"](0)
    b0["prenorm"](2)
    b0["xposeT"](2)
    b0["xposeE"](1)
    b0["prenorm"](3)
    b0["xposeT"](3)
    b0["xposeE"](2)
    b0["xposeE"](3)
    SB_ = blocks[4]
    for bi in range(4):
        cur = blocks[bi]
        nxt = blocks[bi + 1] if bi < 3 else None

        def pre_merge(cur=cur, nxt=nxt, bi=bi):
            if bi == 0:
                gate_part()
                w_out_dma()
            for i in range(2):
                cur["rload"](i)
            if nxt is not None:
                for i in range(2):
                    nxt["prenorm"](i)
                for i in range(2, 4):
                    nxt["xload"](i)
            else:
                SB_["rload"](0)
        if bi < 3:
            if bi == 1:
                g1 = cur["bodygen"](pre_merge)
                next(g1)
                SB_["prenorm"](0)
                next(g1)
                next(g1)
                SB_["xpose"](0)
                next(g1, None)
            elif bi == 0:
                g0_ = cur["bodygen"](pre_merge)
                next(g0_)
                s1_sample()
                small_prep()
                for _ in g0_:
                    pass
            else:
                cur["body"](pre_merge)
            cur["rload"](2)
            cur["rload"](3)
            order = [("x", 0), ("x", 1), ("p", 2), ("p", 3), ("A", 0), ("A", 1), ("x", 2), ("B", 0), ("A", 2), ("x", 3),
                     ("B", 1), ("A", 3), ("B", 2), ("B", 3)]
            for kind, i in order:
                if kind == "x":
                    nxt["xpose"](i)
                elif kind == "p":
                    nxt["prenorm"](i)
                elif kind == "ps":
                    SB_["prenorm"](i)
                elif kind == "xs":
                    SB_["xpose"](i)
                elif kind == "A":
                    cur["tailA"](i)
                else:
                    cur["tailB"](i)
        else:
            SB_["prefix"]()
            g3 = cur["bodygen"](pre_merge)
            gs = SB_["bodygen"](None)
            for _ in range(3):
                next(g3)
                next(gs)
            next(g3, None)
            cur["rload"](2)
            cur["rload"](3)
            next(gs, None)
            for kind, i in [("A", 0), ("AS", 0), ("B", 0), ("A", 1), ("BS", 0), ("A", 2), ("B", 1), ("A", 3), ("B", 2), ("B", 3)]:
                if kind == "A":
                    cur["tailA"](i)
                elif kind == "B":
                    cur["tailB"](i)
                elif kind == "AS":
                    SB_["tailA"](i)
                else:
                    SB_["tailB"](i)
    S.emit()
    return nc, dbg_out


def make_in_maps(inp):
    f = lambda a: np.ascontiguousarray(np.asarray(a, dtype=np.float32))
    ident, band, invc, tril_t, sel = _consts()
    x_prompt, x_sample = f(inp["x_prompt"]), f(inp["x_sample"])
    state_pool, c_prompt, c_sample = f(inp["state_pool"]), f(inp["c_prompt"]), f(inp["c_sample"])
    sgu_w = f(inp["sgu_w"])[0]
    shared = {
        "w_ada": f(inp["w_ada"])[0],
        "b_ada": f(inp["b_ada"]).reshape(1, 3 * D),
        "ng_l": f(f(inp["norm_gain"]).reshape(8, 128).T),
        "w_in": f(inp["w_in"])[0],
        "pw_l": f(f(inp["pool_w"])[0].transpose(1, 0, 2).reshape(128, 512)),
        "pb_l": f(f(inp["pool_b"])[0].T),
        "psc_l": f(f(inp["pool_scale"]).reshape(4, 128).T),
        "psc_row": f(inp["pool_scale"]).reshape(1, 512),
        "lng_row": f(inp["sgu_ln_g"]).reshape(1, 512),
        "lnb_row": f(inp["sgu_ln_b"]).reshape(1, 512),
        "swt_l": f(sgu_w.transpose(2, 0, 1).reshape(128, 512)),
        "sgb_row": f(inp["sgu_b"]).reshape(1, 512),
        "w00_row": f(sgu_w[:, 0, 0]).reshape(1, 4),
        "b0_row": f(f(inp["sgu_b"])[0][:, 0]).reshape(1, 4),
        "w_ba": f(inp["w_branch_a"])[0],
        "w_bb": f(inp["w_branch_b"])[0],
        "w_out": f(inp["w_out"])[0],
        "fg_row": f(inp["final_gain"]).reshape(1, D),
        "k_ident": ident, "k_band": band, "k_invc": invc, "k_tril": tril_t, "k_sel": sel,
    }
    maps = []
    for c in range(8):
        m = dict(shared)
        m["xp"] = f(x_prompt[c])
        m["xs"] = f(x_sample[c * NS:(c + 1) * NS, 0])
        m["sp"] = f(state_pool[0, c * NS:(c + 1) * NS])
        m["cs"] = f(c_sample[c * NS:(c + 1) * NS])
        m["cp"] = f(c_prompt[c:c + 1])
        maps.append(m)
    return maps


_NC = None


def kernel(**inputs):
    global _NC
    if _NC is None:
        _NC = build()[0]
    maps = make_in_maps(inputs)
    res = run_bass_kernel_spmd(_NC, maps, core_ids=list(range(8)))
    rs = res.results
    y_prompt = np.stack([rs[c]["yp"] for c in range(8)], 0).astype(np.float32)
    y_sample = np.concatenate([rs[c]["ys"] for c in range(8)], 0).reshape(128, 1, D).astype(np.float32)
    npp = np.stack([rs[c]["npp"][1:16] for c in range(8)], 0)[None].astype(np.float32)
    nps = np.concatenate([rs[c]["nps"] for c in range(8)], 0)[None].astype(np.float32)
    nvp = np.stack([rs[c]["nvp"] for c in range(8)], 0)[None].astype(np.float32)
    nvs = np.concatenate([rs[c]["nvs"] for c in range(8)], 0).reshape(1, 128, 1, DA).astype(np.float32)
    return (y_prompt, y_sample, npp, nps, nvp, nvs)
```

```python
import numpy as np
import concourse.bass as bass
import concourse.mybir as mybir
from concourse.bass_utils import run_bass_kernel_spmd

F32, BF16 = mybir.dt.float32, mybir.dt.bfloat16
AF = mybir.ActivationFunctionType
ALU = mybir.AluOpType
AX = mybir.AxisListType

D = 1024
T = 2048
NS = 16
DA = 512
DIN = 4608
PBUF = 15
EPS = 1e-6
WINDOWS = (2, 4, 8, 16)
PW = 34
C_A, C_GA, C_U, C_V, C_GB, C_MA, C_MB = 0, 512, 1024, 1536, 2048, 2560, 3584
RING = 8
SBUF_BASE = 16512
SBUF_LIMIT = 229344


class Buf:
    __slots__ = ("name", "w", "r")

    def __init__(self, name):
        self.name = name
        self.w = None
        self.r = []


class Op:
    __slots__ = ("eng", "pos", "fn", "deps", "needed", "seq", "dma", "dsem", "dval", "dprev", "tag", "indep", "sraw")


class Sched:
    ENGS = ("pe", "act", "dve", "pool", "sp")

    def __init__(self, nc, ring=RING):
        self.nc = nc
        self.ops = {e: [] for e in self.ENGS}
        self.sem = {e: nc.alloc_semaphore("s_" + e) for e in self.ENGS}
        self.ring_n = ring
        self.ring = [nc.alloc_semaphore(f"d_sp_{i}") for i in range(ring)]
        self.ndma = {"sp": 0, "pool": 0, "act": 0}
        self.pool_sems = []
        self.log = None

    def add(self, eng, fn, reads=(), writes=(), dma=False, indep=False):
        op = Op()
        op.indep = indep
        op.tag = 0
        op.eng, op.fn, op.dma = eng, fn, dma
        op.pos = len(self.ops[eng])
        op.needed, op.seq = False, None
        op.dsem = op.dval = op.dprev = None
        deps = []
        op.sraw = None
        for b in list(reads) + list(writes):
            if b.w is not None and b.w.eng == eng and not b.w.dma and not dma:
                if op.sraw is None or op.sraw.pos < b.w.pos:
                    op.sraw = b.w
        if op.sraw is not None:
            op.sraw.needed = True
        for b in reads:
            if b.w is not None:
                deps.append(b.w)
        for b in writes:
            if b.w is not None:
                deps.append(b.w)
            deps.extend(b.r)
        seen, ud = set(), []
        for d in deps:
            if id(d) not in seen and d is not op:
                seen.add(id(d))
                ud.append(d)
        best = {}
        pr = []
        for d in ud:
            if d.dma:
                pr.append(d)
            elif d.eng not in best or best[d.eng].pos < d.pos:
                best[d.eng] = d
        ud = pr + list(best.values())
        op.deps = ud
        for d in ud:
            if not (d.eng == "pe" and eng == "pe" and not d.dma):
                d.needed = True
        for b in reads:
            b.r.append(op)
        for b in writes:
            b.w = op
            b.r = []
        if dma:
            i = self.ndma[eng]
            self.ndma[eng] += 1
            if eng == "sp":
                op.dsem = self.ring[i % self.ring_n]
                op.dval = 16 * (i // self.ring_n + 1)
                if i >= self.ring_n:
                    op.dprev = (op.dsem, 16 * (i // self.ring_n))
            else:
                s = self.nc.alloc_semaphore(f"d_{eng}_{i}")
                self.pool_sems.append(s)
                op.dsem, op.dval = s, 16
        self.ops[eng].append(op)
        return op

    def emit(self):
        nc = self.nc
        for e in self.ENGS:
            n = 0
            for op in self.ops[e]:
                if op.needed and not op.dma:
                    n += 1
                    op.seq = n
        engobj = {"pe": nc.tensor, "act": nc.scalar, "dve": nc.vector, "pool": nc.gpsimd, "sp": nc.sync}

        def run(ename, eng):
            waited = {}
            ops = self.ops[ename]
            for op in ops:
                need = {}
                for d in op.deps:
                    if d.dma:
                        key, val = d.dsem, d.dval
                    else:
                        if d.eng == ename:
                            if ename == "pe" or op.indep or op.dma is False:
                                continue
                            if op.dma and False:
                                continue
                        key, val = self.sem[d.eng], d.seq
                    if waited.get(key.num, 0) >= val:
                        continue
                    if need.get(key.num, (None, 0))[1] < val:
                        need[key.num] = (key, val)
                if op.sraw is not None and ename != "pe" and not op.indep and op.pos - op.sraw.pos <= 3:
                    key, val = self.sem[ename], op.sraw.seq
                    if waited.get(key.num, 0) < val and need.get(key.num, (None, 0))[1] < val:
                        need[key.num] = (key, val)
                if op.dprev is not None:
                    key, val = op.dprev
                    if waited.get(key.num, 0) < val and need.get(key.num, (None, 0))[1] < val:
                        need[key.num] = (key, val)
                for k, (key, val) in need.items():
                    eng.wait_ge(key, val)
                    waited[k] = val
                    if self.log is not None:
                        self.log.append(f"{ename} WAIT {key.name}>={val}")
                if self.log is not None:
                    self.log.append(f"{ename} OP line{op.tag} seq={op.seq} dma={(op.dsem.name, op.dval) if op.dma else None}")
                ins = op.fn(eng)
                if op.dma:
                    ins.then_inc(op.dsem, 16)
                elif op.seq is not None:
                    ins.then_inc(self.sem[ename], 1)
            last = {}
            for op in ops:
                if op.dma:
                    last[op.dsem.num] = (op.dsem, op.dval)
            for k, (key, val) in last.items():
                if waited.get(k, 0) < val:
                    eng.wait_ge(key, val)

        with nc.Block() as block:
            @block.tensor
            def _(e):
                run("pe", e)

            @block.scalar
            def _(e):
                run("act", e)

            @block.vector
            def _(e):
                run("dve", e)

            @block.gpsimd
            def _(e):
                run("pool", e)

            @block.sync
            def _(e):
                run("sp", e)


def _consts():
    ident = np.eye(128, dtype=np.float32)
    band = np.zeros((128, 12, 128), np.float32)
    invc = np.zeros((4, 16), np.float32)
    t = np.arange(128)[:, None]
    tp = np.arange(128)[None, :]
    for g, w in enumerate(WINDOWS):
        cur = ((tp - t >= 0) & (tp - t < w)).astype(np.float32)
        band[:, g, :] = cur - w * np.eye(128, dtype=np.float32)
        band[:, 4 + g, :] = (t - 128 > tp - w).astype(np.float32)
        cnt = np.minimum(np.arange(128) + 1, w).astype(np.float32)
        band[:, 8 + g, :] = cur - np.diag(cnt)
        invc[g] = 1.0 / cnt[:16]
    tril_t = (tp >= t).astype(np.float32)
    sel = np.zeros((120, 2, 4, NS), np.float32)
    for half in range(2):
        for tokl in range(8):
            for j in range(PBUF):
                for g, w in enumerate(WINDOWS):
                    if j >= PBUF - (w - 1):
                        sel[tokl * PBUF + j, half, g, half * 8 + tokl] = 1.0
    return ident, band.reshape(128, 12 * 128), invc.reshape(1, 64), tril_t, sel.reshape(120, 128)


def build(stop_after=None, dbg=()):
    nc = bass.Bass("TRN2", target_bir_lowering=False)
    S = Sched(nc)

    def din(name, shape, dt=F32):
        return nc.dram_tensor(name, list(shape), dt, kind="ExternalInput").ap()

    def dout(name, shape, dt=F32):
        return nc.dram_tensor(name, list(shape), dt, kind="ExternalOutput").ap()

    xp = din("xp", [T, D])
    xs = din("xs", [NS, D])
    spd = din("sp", [NS, PBUF, DA])
    cs = din("cs", [NS, D])
    cp = din("cp", [1, D])
    w_ada = din("w_ada", [D, 3 * D])
    b_ada = din("b_ada", [1, 3 * D])
    ng_l = din("ng_l", [128, 8])
    w_in = din("w_in", [D, DIN])
    pw_l = din("pw_l", [128, 512])
    pb_l = din("pb_l", [128, 4])
    psc_l = din("psc_l", [128, 4])
    psc_row = din("psc_row", [1, 512])
    lng_row = din("lng_row", [1, 512])
    lnb_row = din("lnb_row", [1, 512])
    swt_l = din("swt_l", [128, 512])
    sgb_row = din("sgb_row", [1, 512])
    w00_row = din("w00_row", [1, 4])
    b0_row = din("b0_row", [1, 4])
    w_ba = din("w_ba", [DA, D])
    w_bb = din("w_bb", [DA, D])
    w_out = din("w_out", [D, D])
    fg_row = din("fg_row", [1, D])
    k_ident = din("k_ident", [128, 128])
    k_band = din("k_band", [128, 12 * 128])
    k_invc = din("k_invc", [1, 64])
    k_tril = din("k_tril", [128, 128])
    k_sel = din("k_sel", [120, 128])

    yp = dout("yp", [T, D])
    ys = dout("ys", [NS, D])
    npp = dout("npp", [16, DA])
    nps = dout("nps", [NS, PBUF, DA])
    nvp = dout("nvp", [128, DA])
    nvs = dout("nvs", [NS, DA])
    dbg_out = {}

    off = [SBUF_BASE]
    sizes = {F32: 4, BF16: 2}

    def nbytes(shape, dt):
        n = sizes[dt]
        for s in shape[1:]:
            n *= s
        return (n + 31) // 32 * 32

    def sb(name, shape, dt, at=None):
        if at is None:
            at = off[0]
            off[0] += nbytes(shape, dt)
            assert off[0] <= SBUF_LIMIT, (name, off[0])
        return nc.alloc_sbuf_tensor_at(name, list(shape), dt, offset=at), at

    def SB(name, shape, dt):
        return sb(name, shape, dt)[0]

    w_in_bf = SB("w_in_bf", [128, 8, DIN], BF16)
    w_ba_bf = SB("w_ba_bf", [128, 4, D], BF16)
    w_bb_bf = SB("w_bb_bf", [128, 4, D], BF16)
    w_out_bf, o_wout = sb("w_out_bf", [128, 8, D], BF16)
    wag = [sb("wag0", [128, 8, 512], BF16, at=o_wout)[0], sb("wag1", [128, 8, 512], BF16, at=o_wout + 8192)[0]]
    ident_f = SB("ident_f", [128, 128], F32)
    ident_b = SB("ident_b", [128, 128], BF16)
    band_bf = SB("band_bf", [128, 12, 128], BF16)
    invc = SB("invc", [128, 4, 16], F32)
    WtT = SB("WtT", [128, 4, 128], BF16)
    Dg = SB("Dg", [NS, 4, NS], BF16)
    pool_w_bf = SB("pool_w_bf", [128, 4, 128], BF16)
    pb2 = SB("pb2", [128, 4], F32)
    pb_t = SB("pb_t", [128, 4], F32)
    psc_t = SB("psc_t", [128, 4], F32)
    ng = SB("ng", [128, 8], F32)
    lng_bc = SB("lng_bc", [128, 512], F32)
    lnb_bc = SB("lnb_bc", [128, 512], F32)
    fg_bc = SB("fg_bc", [128, D], F32)
    gateB = SB("gateB", [128, D], F32)
    gate_s = SB("gate_s", [PW, D], F32)
    brow = SB("brow", [65, 4, 128], BF16)
    ones_bf = SB("ones_bf", [65, 128], BF16)
    ones_f, o_onesf = sb("ones_f", [33, 128], F32)
    negh = SB("negh", [128, 1], F32)
    modT = SB("modT", [128, 16, PW], F32)
    s1all = SB("s1all", [128, 8, PW], F32)
    s1p = SB("s1p", [128, 8], F32)
    b0_bc = SB("b0_bc", [128, 4], F32)
    w00_bc = SB("w00_bc", [NS, 4], F32)
    ss = [SB(f"ss{i}", [128, 1], F32) for i in range(2)]
    msq = [SB(f"msq{i}", [128, 1], F32) for i in range(2)]
    rr = [SB(f"rr{i}", [128, 1], F32) for i in range(2)]
    st6 = [SB(f"st6{i}", [128, 6], F32) for i in range(2)]
    mv = [SB(f"mv{i}", [128, 2], F32) for i in range(2)]
    vtmp = [SB(f"vtmp{i}", [128, 1], F32) for i in range(2)]
    rstd = [SB(f"rstd{i}", [128, 1], F32) for i in range(2)]
    ss2 = [SB(f"ss2{i}", [128, 1], F32) for i in range(2)]
    ms2 = [SB(f"ms2{i}", [128, 1], F32) for i in range(2)]
    r2 = [SB(f"r2{i}", [128, 1], F32) for i in range(2)]
    hT, o_hT = sb("hT", [128, 8, 512], BF16)
    a_bf, o_abf = sb("a_bf", [128, 4, 512], BF16)
    a_prev, o_aprev = sb("a_prev", [128, 512], BF16)
    vn, o_vn = sb("vn", [128, 4, 512], BF16)
    ug, o_ug = sb("ug", [128, 4, 512], BF16)
    sga, o_sga = sb("sga", [128, 4, 512], BF16)
    sgb_o = off[0]
    sgb = [SB(f"sgb{i}", [128, 512], BF16) for i in range(2)]
    pooledT = [SB(f"pooledT{i}", [128, 512], BF16) for i in range(2)]
    mergedT, o_mg = sb("mergedT", [128, 8, 512], BF16)
    xb_o = []
    xb = []
    for i in range(2):
        t_, o_ = sb(f"xb{i}", [128, D], F32)
        xb.append(t_)
        xb_o.append(o_)
    xr_o = off[0]
    xr = [SB(f"xr{i}", [128, D], F32) for i in range(2)]
    xn_o = off[0]
    xn = [SB(f"xn{i}", [128, D], BF16) for i in range(2)]
    jk_o = off[0]
    jk = SB("jk", [128, D], BF16)
    xn2 = [jk, jk]
    vg_o = []
    vg = []
    for i in range(2):
        t_, o_ = sb(f"vg{i}", [128, 512], F32)
        vg.append(t_)
        vg_o.append(o_)
    tt_o = []
    tt = []
    for i in range(2):
        t_, o_ = sb(f"tt{i}", [128, 512], F32)
        tt.append(t_)
        tt_o.append(o_)
    sma_o = off[0]
    sma = [SB(f"sma{i}", [128, 512], BF16) for i in range(2)]
    alast = None
    smb = [SB(f"smb{i}", [128, 512], BF16) for i in range(2)]
    t1_o = []
    t1 = []
    for i in range(2):
        t_, o_ = sb(f"t1{i}", [128, 512], BF16)
        t1.append(t_)
        t1_o.append(o_)
    t2 = [SB(f"t2{i}", [128, 512], BF16) for i in range(2)]
    cT_o = off[0]
    hT_s = SB("hT_s", [128, 8, NS], BF16)
    ug_s = SB("ug_s", [128, 4, NS], BF16)
    sga_s = SB("sga_s", [128, 4, NS], BF16)
    mergedT_s = SB("mergedT_s", [128, 8, NS], BF16)
    print("SBUF bytes/partition used:", off[0])

    xr_o0 = None
    wa = [sb("wa0", [128, 8, 512], BF16, at=o_mg)[0], sb("wa1", [128, 8, 512], BF16, at=o_hT)[0]]
    wa.append(sb("wa2", [128, 8, 512], BF16, at=xr_o)[0])
    wa.append(sb("wa3", [128, 8, 512], BF16, at=sma_o)[0])
    mod_sb = sb("mod_sb", [PW, 3 * D], F32, at=o_vn)[0]
    assert o_ug == o_vn + 4096 and o_sga == o_ug + 4096
    cin = sb("cin", [PW, D], F32, at=o_abf)[0]
    ba = [sb("ba0", [PW, 512], F32, at=vg_o[0])[0], sb("ba1", [PW, 512], F32, at=vg_o[1])[0],
          sb("ba2", [PW, 512], F32, at=tt_o[0])[0], sb("ba3", [PW, 512], F32, at=tt_o[1])[0]]
    stg_pw = sb("stg_pw", [128, 512], F32, at=sgb_o)[0]
    stg_psc = sb("stg_psc", [128, 512], F32, at=sgb_o + 2048)[0]
    stg_sw = sb("stg_sw", [128, 512], F32, at=jk_o)[0]
    stg_tril = sb("stg_tril", [128, 128], F32, at=o_aprev)[0]
    browf = sb("browf", [65, 512], F32, at=xn_o)[0]
    browh = sb("browh", [65, 512], BF16, at=xn_o + 2048)[0]
    pfx = sb("pfx", [120, 2, 512], F32, at=xr_o)[0]
    a_s = sb("a_s", [NS, 512], F32, at=xr_o + 4096)[0]
    vn_s = sb("vn_s", [NS, 1, 512], BF16, at=xr_o + 6144)[0]
    sel_sb = sb("sel_sb", [120, 128], F32, at=o_onesf)[0]
    psumT = invc
    tmpS = sb("tmpS", [128, 8, NS], F32, at=tt_o[1])[0]

    banks = [nc.alloc_psum_tensor(f"pbk{i}", [128, 512], F32) for i in range(6)]
    psT = [nc.alloc_psum_tensor(f"ptr{i}", [128, 1024], BF16) for i in range(2)]
    Bbank = [Buf(f"bank{i}") for i in range(6)]
    BpsT = [Buf(f"psT{i}") for i in range(2)]
    bank_ctr = [0]

    def nextbank():
        i = bank_ctr[0] % 6
        bank_ctr[0] += 1
        return banks[i], Bbank[i]

    B = {}

    def bf(name):
        if name not in B:
            B[name] = Buf(name)
        return B[name]

    Bw_in = [bf(f"w_in{g}") for g in range(9)]
    B_abf, B_aprev, B_vn, B_ug, B_sga, B_mg = (bf(n) for n in ("a_bf", "a_prev", "vn", "ug", "sga", "mergedT"))
    B_hT = [bf(f"hT{i}") for i in range(4)]
    B_xb = [bf(f"xb{i}") for i in range(2)]
    B_xr = [bf(f"xr{i}") for i in range(2)]
    B_xn = [bf(f"xn{i}") for i in range(2)]
    B_vg = [bf(f"vg{i}") for i in range(2)]
    B_tt = [bf(f"tt{i}") for i in range(2)]
    B_t1 = [bf(f"t1{i}") for i in range(2)]
    B_t2 = [bf(f"t2{i}") for i in range(2)]
    B_sma = [bf(f"sma{i}") for i in range(2)]
    B_smb = [bf(f"smb{i}") for i in range(2)]
    B_sgb = [bf(f"sgb{i}") for i in range(2)]
    B_pT = [bf(f"pooledT{i}") for i in range(2)]
    B_mod = [B_vn, B_ug, B_sga]
    xr.append(sb("xr2", [128, D], F32, at=o_ug)[0])
    xr.append(sb("xr3", [128, D], F32, at=o_sga)[0])
    B_xr.extend([B_ug, B_sga])
    alast = sb("alast", [128, 512], F32, at=tt_o[0])[0]
    B_alast = [B_tt[0]]
    B_xn2 = [bf("jk"), bf("jk")]

    def dma(q, out, in_, r, w):
        return S.add(q, lambda e, o=out, i=in_: e.dma_start(out=o, in_=i), r, w, dma=True)

    def mm(out, lhsT, rhs, start, stop, r, w):
        return S.add("pe", lambda e, o=out, l=lhsT, rh=rhs, s0=start, s1=stop: e.matmul(o, l, rh, start=s0, stop=s1), r, w)

    def tr(out, in_, ident, r, w):
        return S.add("pe", lambda e, o=out, i=in_, idn=ident: e.transpose(o, i, idn), r, w)

    def act(out, in_, func, r, w, **kw):
        return S.add("act", lambda e, o=out, i=in_, f=func, k=kw: e.activation(o, i, f, **k), r, w)

    def ts(eng, out, in0, s1, s2, op0, op1, r, w, indep=False):
        if s2 is None:
            return S.add(eng, lambda e, o=out, i=in0, a=s1, p0=op0: e.tensor_scalar(o, i, a, None, p0), r, w, indep=indep)
        return S.add(eng, lambda e, o=out, i=in0, a=s1, b=s2, p0=op0, p1=op1: e.tensor_scalar(o, i, a, b, p0, p1), r, w,
                     indep=indep)

    def tten(eng, out, in0, in1, op, r, w):
        return S.add(eng, lambda e, o=out, a=in0, b=in1, p=op: e.tensor_tensor(o, a, b, p), r, w)

    def stt(out, in0, scalar, in1, op0, op1, r, w):
        return S.add("dve", lambda e, o=out, a=in0, s=scalar, b=in1, p0=op0, p1=op1: e.scalar_tensor_tensor(o, a, s, b, p0, p1), r, w)

    def cpy(eng, out, in_, r, w):
        return S.add(eng, lambda e, o=out, i=in_: e.tensor_copy(o, i), r, w)

    def mset(eng, ap, val, w):
        return S.add(eng, lambda e, a=ap, v=val: e.memset(a, v), (), w)

    def dump(name, ap, shape, bufs, dt=BF16):
        if name not in dbg:
            return
        d = dout("dbg_" + name, shape, dt)
        dbg_out[name] = shape
        dma("sp", d, ap, bufs, [bf("dbg_" + name)])

    w_ada_v = w_ada.rearrange("(k p) n -> p k n", p=128)
    w_in_v = w_in.rearrange("(k p) n -> p k n", p=128)
    B_wa = [[B_mg], B_hT, [B_xr[0], B_xr[1]], B_sma + B_smb + B_t1 + B_t2]
    B_ba = [B_vg[0], B_vg[1], B_tt[0], B_tt[1]]
    B_spw, B_spsc = [B_sgb[0], B_sgb[1]], [B_pT[0], B_pT[1]]
    NWA = 4
    cT = sb("cT", [128, 8, PW], BF16, at=cT_o)[0]
    B_cT = [bf("hT_s"), bf("ug_s"), bf("sga_s"), bf("mergedT_s")]

    chain = []
    DEPTH = 13

    def cdma(out, in_, w, extra_r=(), depth=None):
        depth = DEPTH if depth is None else depth
        r = list(extra_r)
        if len(chain) >= depth:
            r.append(chain[-depth])
        t = bf(f"chain{len(chain)}")
        chain.append(t)
        dma("pool", out, in_, r, list(w) + [t])

    def wdma(g, extra_r=()):
        cdma(w_in_bf[:, :, g * 512:(g + 1) * 512], w_in_v[:, :, g * 512:(g + 1) * 512], [Bw_in[g]], extra_r)

    wq = []

    def wq_pop(n):
        for _ in range(n):
            if wq:
                wq.pop(0)()

    def setup_a():
        dma("sp", ident_f[:, :], k_ident, [], [bf("ident_f")])
        mset("pool", negh[:, :], -0.5, [bf("negh")])
        mset("pool", cin[:, :], 0.0, [B_abf])
        dma("sp", cin[0:NS, :], cs, [], [B_abf])
        dma("sp", cin[32:33, :], cp, [], [B_abf])
        blocks[0]["xload"](0)
        blocks[0]["xload"](1)
        mset("dve", browf[:, :], 0.0, [B_xn[0]])
        dma("sp", browf[0:1, :], sgb_row, [], [B_xn[0]])
        dma("sp", browf[32:33, :], sgb_row, [], [B_xn[0]])
        dma("sp", browf[64:65, :], sgb_row, [], [B_xn[0]])
        dma("sp", ng[:, :], ng_l, [], [bf("ng")])
        for j in range(4):
            dma("sp", ba[j][:, :], b_ada[0:1, j * 512:(j + 1) * 512].partition_broadcast(PW), [], [B_ba[j]])
        for j in range(2):
            cdma(wa[j][:, :, :], w_ada_v[:, :, j * 512:(j + 1) * 512], B_wa[j])
        dma("pool", ident_b[:, :], k_ident, [], [bf("ident_b")])
        dma("pool", band_bf[:, :, :], k_band.rearrange("p (i t) -> p i t", t=128), [], [bf("band")])
        for j in range(2, NWA):
            cdma(wa[j][:, :, :], w_ada_v[:, :, j * 512:(j + 1) * 512], B_wa[j], depth=2)
        dma("sp", invc[:, :, :], k_invc.rearrange("o (g t) -> o g t", t=16).partition_broadcast(128), [], [bf("invc")])
        dma("sp", stg_pw[:, :], pw_l, [], B_spw)
        dma("sp", stg_psc[:, :], psc_row.partition_broadcast(128), [], B_spsc)
        dma("sp", pb_t[:, :], pb_l, [], [bf("pb_t")])
        dma("sp", psc_t[:, :], psc_l, [], [bf("psc_t")])
        dma("sp", lng_bc[:, :], lng_row.partition_broadcast(128), [], [bf("lng")])
        dma("sp", lnb_bc[:, :], lnb_row.partition_broadcast(128), [], [bf("lnb")])
        dma("sp", fg_bc[:, :], fg_row.partition_broadcast(128), [], [bf("fg")])
        dma("sp", b0_bc[:, :], b0_row.partition_broadcast(128), [], [bf("b0")])
        dma("sp", w00_bc[:, :], w00_row.partition_broadcast(NS), [], [bf("w00")])
        dma("sp", stg_sw[:, :], swt_l, [], [bf("jk")])
        dma("sp", stg_tril[:, :], k_tril, [], [B_aprev])

        mset("dve", ones_bf[:, :], 1.0, [bf("ones_bf")])
        mset("dve", ones_f[:, :], 1.0, [bf("ones_f")])
        browv = brow[:, :, :].rearrange("p g t -> p (g t)")
        cpy("dve", browh[:, :], browf[:, :], [B_xn[0]], [B_xn[1]])
        cpy("dve", browv[0:32, :], browh[0:32, :], [B_xn[1]], [bf("brow")])
        tten("dve", browf[32:64, :], browf[32:64, :], browh[32:64, :], ALU.subtract,
             [B_xn[0], B_xn[1]], [B_xn[0]])
        tten("dve", browf[64:65, :], browf[64:65, :], browh[64:65, :], ALU.subtract,
             [B_xn[0], B_xn[1]], [B_xn[0]])
        cpy("dve", browv[32:64, :], browf[32:64, :], [B_xn[0]], [bf("brow")])
        cpy("dve", browh[64:65, :], browf[64:65, :], [B_xn[0]], [B_xn[1]])
        tten("dve", browv[64:65, :], browf[64:65, :], browh[64:65, :], ALU.subtract,
             [B_xn[0], B_xn[1]], [bf("brow")])

    def small_prep():
        tten("dve", pool_w_bf[:, :, :].rearrange("p g d -> p (g d)"), stg_pw[:, :], stg_psc[:, :], ALU.mult,
             B_spw + B_spsc, [bf("pool_w")])
        tten("dve", pb2[:, :], pb_t[:, :], psc_t[:, :], ALU.mult, [bf("pb_t"), bf("psc_t")], [bf("pb2")])
        for g in range(4):
            tten("dve", WtT[:, g, :], stg_sw[:, g * 128:(g + 1) * 128], stg_tril[:, :], ALU.mult,
                 [bf("jk"), B_aprev], [bf("WtT")])
            ts("dve", Dg[:, g, :], ident_f[0:NS, 0:NS], w00_bc[:, g:g + 1], None, ALU.mult, None,
               [bf("ident_f"), bf("w00")], [bf("Dg")])

    def build_wq():
        for g in (0, 3):
            wdma(g)

    def build_wq2():
        for g in (2, 1, 4):
            wdma(g)
        for j in (4, 5):
            cdma(wag[j - 4][:, :, :], w_ada_v[:, :, j * 512:(j + 1) * 512], [bf(f"w_out_g{j - 4}")])
        for g in (5, 7):
            wdma(g)
        cdma(w_ba_bf[:, :, :], w_ba.rearrange("(k p) n -> p k n", p=128), [bf("w_ba")])
        cdma(w_bb_bf[:, :, :], w_bb.rearrange("(k p) n -> p k n", p=128), [bf("w_bb")])
        for g in (6, 8):
            wdma(g)
        wq.append(lambda: [dma("sp", ba[j - 2][:, :], b_ada[0:1, j * 512:(j + 1) * 512].partition_broadcast(PW), [],
                               [B_ba[j - 2]]) for j in (4, 5)])

    def w_out_dma():
        cdma(w_out_bf[:, :, :], w_out.rearrange("(k p) n -> p k n", p=128), [bf("w_out"), bf("w_out_g0"), bf("w_out_g1")])

    def mod_compute(j, wi, bi, dest, dtags, wsrc=None, wtags=None):
        wsrc = wa[wi] if wsrc is None else wsrc
        wtags = B_wa[wi] if wtags is None else wtags
        bk, Bk = nextbank()
        for k in range(8):
            mm(bk[0:PW, :], cT[:, k, :], wsrc[:, k, :], k == 0, k == 7, B_cT + wtags, [Bk])
        tten("dve", dest, bk[0:PW, :], ba[bi][:, :], ALU.add, [Bk, B_ba[bi]], dtags)

    def gate_part():
        for j in (4, 5):
            mod_compute(j, j - 2, j - 2, gate_s[:, (j - 4) * 512:(j - 3) * 512], [bf("gate_s")],
                        wsrc=wag[j - 4], wtags=[bf(f"w_out_g{j - 4}")])
        for hh in range(2):
            bk, Bk = nextbank()
            mm(bk[:, :], ones_f[32:33, :], gate_s[32:33, hh * 512:(hh + 1) * 512], True, True,
               [bf("gate_s"), bf("ones_f")], [Bk])
            cpy("dve", gateB[:, hh * 512:(hh + 1) * 512], bk[:, :], [Bk], [bf("gateB")])

    def setup_b():
        act(cin[:, :], cin[:, :], AF.Silu, [B_abf], [B_abf])
        bk, Bk = nextbank()
        for k in range(8):
            tr(bk[:, k * PW:(k + 1) * PW], cin[0:PW, k * 128:(k + 1) * 128], ident_f[0:PW, 0:PW],
               [B_abf, bf("ident_f")], [Bk])
        cpy("dve", cT[:, :, :], bk[:, 0:8 * PW].rearrange("p (k c) -> p k c", c=PW), [Bk], B_cT)

    def setup_b2():
        for hh in range(2):
            for j in (2 * hh, 2 * hh + 1):
                mod_compute(j, j, j, mod_sb[:, j * 512:(j + 1) * 512], B_mod)
            bk, Bk = nextbank()
            for c in range(8):
                cc = hh * 8 + c
                tr(bk[:, c * PW:(c + 1) * PW], mod_sb[0:PW, cc * 128:(cc + 1) * 128], ident_f[0:PW, 0:PW],
                   B_mod + [bf("ident_f")], [Bk])
            cpy("dve", modT[:, hh * 8:(hh + 1) * 8, :], bk[:, 0:8 * PW].rearrange("p (k c) -> p k c", c=PW),
                [Bk], [bf("modT")])
        stt(s1p[:, :], modT[:, 8:16, 32], 1.0, ng[:, :], ALU.add, ALU.mult, [bf("modT"), bf("ng")], [bf("s1p")])

    def s1_sample():
        for k in range(8):
            ts("dve", s1all[:, k, :], modT[:, 8 + k, :], 1.0, ng[:, k:k + 1], ALU.add, ALU.mult,
               [bf("modT"), bf("ng")], [bf("s1all")])

    cnt = {"x": 0, "v": 0, "m": 0, "o": 0, "g": 0}

    def rmsnorm_stats(src, P, Bsrc, i2, junk, Bjunk, ssl, msl, rl, tag):
        act(junk, src, AF.Square, [Bsrc], [Bjunk, bf(f"{tag}ss{i2}")], accum_out=ssl[i2][0:P, :])
        ts("pool", msl[i2][0:P, :], ssl[i2][0:P, :], 1.0 / D, EPS, ALU.mult, ALU.add,
           [bf(f"{tag}ss{i2}")], [bf(f"{tag}ms{i2}")])
        tten("pool", rl[i2][0:P, :], msl[i2][0:P, :], negh[0:P, :], ALU.pow,
             [bf(f"{tag}ms{i2}"), bf("negh")], [bf(f"{tag}r{i2}")])

    def zchunk_fm(hsrc, Bh, col0, NT):
        bk, Bk = nextbank()
        g = col0 // 512
        for k in range(8):
            mm(bk[:, 0:NT], w_in_bf[:, k, col0:col0 + 128], hsrc[:, k, 0:NT], k == 0, k == 7,
               [Bw_in[g]] + list(dict.fromkeys(Bh)), [Bk])
        return bk, Bk

    def ztile_tm(hsrc, Bh, col0, t0, P):
        bk, Bk = nextbank()
        g = col0 // 512
        for k in range(8):
            mm(bk[0:P, :], hsrc[:, k, t0:t0 + P], w_in_bf[:, k, col0:col0 + 512], k == 0, k == 7,
               [Bw_in[g], Bh[t0 // 128]], [Bk])
        return bk, Bk

    def mk_block(b, sample):
        rb = b * 512
        NT = NS if sample else 512
        tiles = [(0, NS)] if sample else [(i * 128, 128) for i in range(4)]
        first = (b == 0) and not sample
        last = (b == 3) and not sample
        if sample:
            hT_, B_hT_ = hT_s, [bf("hT_s")] * 4
            ug_, B_ug_ = ug_s, bf("ug_s")
            sga_, B_sga_ = sga_s, bf("sga_s")
            mg_, B_mg_ = mergedT_s, bf("mergedT_s")
        else:
            hT_, B_hT_, ug_, B_ug_, sga_, B_sga_, mg_, B_mg_ = hT, B_hT, ug, B_ug, sga, B_sga, mergedT, B_mg
        xrl, B_xrl = xr, B_xr
        if sample:
            Ba_, Bap_, vn_, B_vn_ = B_xr[1], B_xr[1], vn_s, B_xr[1]
        else:
            Ba_, Bap_, vn_, B_vn_ = B_abf, B_aprev, vn, B_vn
        xsel = {}

        def xload(i):
            t0, P = tiles[i]
            xi = cnt["x"] % 2
            cnt["x"] += 1
            xsel[i] = xi
            if sample:
                dma("sp", xb[xi][0:P, :], xs, [], [B_xb[xi]])
            else:
                q = "act" if (b == 0 and i >= 2) else "sp"
                dma(q, xb[xi][:, :], xp[rb + t0:rb + t0 + P, :], [], [B_xb[xi]])

        def prenorm(i):
            t0, P = tiles[i]
            if i not in xsel:
                xload(i)
            xi = xsel[i]
            xt, Bxt = xb[xi], B_xb[xi]
            rmsnorm_stats(xt[0:P, :], P, Bxt, xi, xn[xi][0:P, :], B_xn[xi], ss, msq, rr, "n")
            ts("dve", xn[xi][0:P, :], xt[0:P, :], rr[xi][0:P, 0:1], None, ALU.mult, None,
               [Bxt, bf(f"nr{xi}")], [B_xn[xi]])

        def xposeT(i):
            t0, P = tiles[i]
            xi = xsel[i]
            for k in range(8):
                tr(psT[xi][:, k * P:(k + 1) * P], xn[xi][0:P, k * 128:(k + 1) * 128], ident_b[0:P, 0:P],
                   [B_xn[xi], bf("ident_b")], [BpsT[xi]])

        def xposeE(i):
            t0, P = tiles[i]
            xi = xsel[i]
            if sample:
                pv = psT[xi][:, 0:8 * P].rearrange("p (k t) -> p k t", t=P)
                tten("dve", tmpS[:, :, :], pv, s1all[:, :, 0:NS], ALU.mult, [BpsT[xi], bf("s1all")], [B_tt[1]])
                tten("dve", hT_[:, :, 0:NS], tmpS[:, :, :], modT[:, 0:8, 0:NS], ALU.add,
                     [B_tt[1], bf("modT")], [B_hT_[0]])
            else:
                for k in range(8):
                    ts("dve", hT_[:, k, t0:t0 + P], psT[xi][:, k * P:(k + 1) * P], s1p[:, k:k + 1],
                       modT[:, k, 32:33], ALU.mult, ALU.add, [BpsT[xi], bf("s1p"), bf("modT")], [B_hT_[i]],
                       indep=(k >= 1))

        def xpose(i):
            xposeT(i)
            xposeE(i)

        def prefix():
            dma("sp", nps[:, 0:14, :], spd[:, 1:15, :], [], [bf("o_nps0")])
            spd2 = spd.rearrange("t j c -> (t j) c")
            for h in range(2):
                dma("sp", pfx[:, h, :], spd2[h * 120:(h + 1) * 120, :], [], [B_xr[0]])
            dma("sp", sel_sb[:, :], k_sel, [], [bf("ones_f")])

        def body(pre_merge=None):
            for _ in _body(pre_merge):
                pass

        def _body(pre_merge):
          if True:
            if b == 0 and not sample:
                dump("hT", hT_[:, :, :], [128, 8, 512], B_hT_)
            def a_tile(i, t0, P):
                bk, Bk = ztile_tm(hT_, B_hT_, C_A, t0, P)
                if sample:
                    cpy("dve", a_s[:, :], bk[0:P, :], [Bk], [Ba_])
                    dma("sp", nps[:, 14, :], a_s[:, :], [Ba_], [bf("o_nps1")])
                else:
                    act(a_bf[:, i, :], bk[:, :], AF.Copy, [Bk], [Ba_])
                    if last and i == 3:
                        bk2, Bk2 = ztile_tm(hT_, B_hT_, C_A, t0, P)
                        cpy("dve", alast[:, :], bk2[:, :], [Bk2], B_alast)
                        dma("sp", npp, alast[112:128, :], B_alast, [bf("o_npp")])
            def ln_tile(i, t0, P, f32_out):
                vi = cnt["v"] % 2
                cnt["v"] += 1
                bk, Bk = ztile_tm(hT_, B_hT_, C_V, t0, P)
                act(vg[vi][0:P, :], bk[0:P, :], AF.Gelu, [Bk], [B_vg[vi]])
                S.add("dve", lambda e, o=st6[vi][0:P, :], a=vg[vi][0:P, :]: e.bn_stats(o, a), [B_vg[vi]], [bf(f"st6{vi}")])
                S.add("dve", lambda e, o=mv[vi][0:P, :], a=st6[vi][0:P, :]: e.bn_aggr(o, a), [bf(f"st6{vi}")], [bf(f"mv{vi}")])
                ts("pool", vtmp[vi][0:P, :], mv[vi][0:P, 1:2], EPS, 1.0, ALU.add, ALU.mult, [bf(f"mv{vi}")], [bf(f"vtmp{vi}")])
                tten("pool", rstd[vi][0:P, :], vtmp[vi][0:P, :], negh[0:P, :], ALU.pow,
                     [bf(f"vtmp{vi}"), bf("negh")], [bf(f"rstd{vi}")])
                stt(vg[vi][0:P, :], vg[vi][0:P, :], mv[vi][0:P, 0:1], lng_bc[0:P, :], ALU.subtract, ALU.mult,
                    [B_vg[vi], bf(f"mv{vi}"), bf("lng")], [B_vg[vi]])
                if f32_out:
                    stt(vg[vi][0:P, :], vg[vi][0:P, :], rstd[vi][0:P, 0:1], lnb_bc[0:P, :], ALU.mult, ALU.add,
                        [B_vg[vi], bf(f"rstd{vi}"), bf("lnb")], [B_vg[vi]])
                    dma("sp", nvs if sample else nvp, vg[vi][0:P, :], [B_vg[vi]], [bf("o_nv" + ("s" if sample else "p"))])
                else:
                    stt(vn_[0:P, i, :], vg[vi][0:P, :], rstd[vi][0:P, 0:1], lnb_bc[0:P, :], ALU.mult, ALU.add,
                        [B_vg[vi], bf(f"rstd{vi}"), bf("lnb")], [B_vn_])

            for i, (t0, P) in enumerate(tiles):
                if i in (1, 3):
                    wq_pop(2)
                a_tile(i, t0, P)
                ln_tile(i, t0, P, False)
                if sample or (last and i == 3):
                    ln_tile(i, t0, P, True)
            if b == 0 and not sample:
                dump("vn", vn_[:, :, :], [128, 4, 512], [B_vn_])
                dump("a_bf", a_bf[:, :, :], [128, 4, 512], [Ba_])
            yield
            for j in range(4):
                wq_pop(1)
                bk, Bk = zchunk_fm(hT_, B_hT_, C_U + j * 128, NT)
                act(ug_[:, j, 0:NT], bk[:, 0:NT], AF.Gelu, [Bk], [B_ug_])
            if sample:
                bkp, Bkp = nextbank()
                for g, w in enumerate(WINDOWS):
                    for h in range(2):
                        mm(bkp[:, g * NS:(g + 1) * NS], pfx[:, h, g * 128:(g + 1) * 128],
                           sel_sb[:, (h * 4 + g) * NS:(h * 4 + g + 1) * NS], h == 0, h == 1,
                           [B_xr[0], bf("ones_f")], [Bkp])
                for g, w in enumerate(WINDOWS):
                    ts("dve", psumT[:, g, :], bkp[:, g * NS:(g + 1) * NS], 1.0 / w, None, ALU.mult, None,
                       [Bkp], [bf("invc")])
            for g, w in enumerate(WINDOWS):
                pi = cnt["g"] % 2
                cnt["g"] += 1
                bk, Bk = nextbank()
                if sample:
                    bka_, Bka_ = zchunk_fm(hT_, B_hT_, C_A + g * 128, NT)
                    stt(pooledT[pi][:, 0:NS], bka_[:, 0:NS], 1.0 / w - 1.0, psumT[:, g, :], ALU.mult, ALU.add,
                        [Bka_, bf("invc")], [B_pT[pi]])
                else:
                    for i in range(4):
                        seq_first = first and i == 0
                        o = bk[:, i * 128:(i + 1) * 128]
                        mm(o, a_bf[:, i, g * 128:(g + 1) * 128], band_bf[:, (8 + g) if seq_first else g, :], True, seq_first,
                           [Ba_, bf("band")], [Bk])
                        if not seq_first:
                            if i == 0:
                                mm(o, a_prev[:, g * 128:(g + 1) * 128], band_bf[:, 4 + g, :], False, True,
                                   [Bap_, bf("band")], [Bk])
                            else:
                                mm(o, a_bf[:, i - 1, g * 128:(g + 1) * 128], band_bf[:, 4 + g, :], False, True,
                                   [Ba_, bf("band")], [Bk])
                    act(pooledT[pi][:, :], bk[:, :], AF.Copy, [Bk], [B_pT[pi]], scale=1.0 / w)
                bk2, Bk2 = zchunk_fm(hT_, B_hT_, C_GA + g * 128, NT)
                act(sga_[:, g, 0:NT], bk2[:, 0:NT], AF.Silu, [Bk2], [B_sga_])
                bk3, Bk3 = nextbank()
                mm(bk3[:, 0:NT], pool_w_bf[:, g, :], pooledT[pi][:, 0:NT], True, True, [bf("pool_w"), B_pT[pi]], [Bk3])
                if first:
                    stt(bk3[:, 0:16], bk3[:, 0:16], float(w), invc[:, g, :], ALU.mult, ALU.mult,
                        [Bk3, bf("invc")], [Bk3])
                stt(sga_[:, g, 0:NT], bk3[:, 0:NT], pb2[:, g:g + 1], sga_[:, g, 0:NT], ALU.add, ALU.mult,
                    [Bk3, bf("pb2"), B_sga_], [B_sga_])
            if not sample:
                cpy("dve", a_prev[:, :], a_bf[:, 3, :], [Ba_], [Bap_])
            if b == 0 and not sample:
                dump("yaT", sga_[:, :, :], [128, 4, 512], [B_sga_])
            yield
            for g in range(4):
                si = cnt["g"] % 2
                cnt["g"] += 1
                wq_pop(1)
                bk, Bk = zchunk_fm(hT_, B_hT_, C_GB + g * 128, NT)
                act(sgb[si][:, 0:NT], bk[:, 0:NT], AF.Silu, [Bk], [B_sgb[si]])
                tten("pool", ug_[:, g, 0:NT], ug_[:, g, 0:NT], sgb[si][:, 0:NT], ALU.mult, [B_ug_, B_sgb[si]], [B_ug_])
                bk2, Bk2 = nextbank()
                if sample:
                    mm(bk2[:, 0:NS], vn_[0:NS, 0, g * 128:(g + 1) * 128], Dg[:, g, :], True, True, [B_vn_, bf("Dg")], [Bk2])
                    stt(ug_[:, g, 0:NS], bk2[:, 0:NS], b0_bc[:, g:g + 1], ug_[:, g, 0:NS], ALU.add, ALU.mult,
                        [Bk2, bf("b0"), B_ug_], [B_ug_])
                else:
                    mm(bk2[:, :], ones_bf[0:65, :], brow[0:65, g:g + 1, :].to_broadcast([65, 4, 128]), True, False,
                       [bf("ones_bf"), bf("brow")], [Bk2])
                    for i in range(4):
                        o = bk2[:, i * 128:(i + 1) * 128]
                        mm(o, vn_[:, i, g * 128:(g + 1) * 128], WtT[:, g, :], False, i == 3, [B_vn_, bf("WtT")], [Bk2])
                    tten("dve", ug_[:, g, :], bk2[:, :], ug_[:, g, :], ALU.mult, [Bk2, B_ug_], [B_ug_])
            if b == 0 and not sample:
                dump("ybT", ug_[:, :, :], [128, 4, 512], [B_ug_])
            yield
            if pre_merge is not None:
                pre_merge()
            wq_pop(8)
            for i in range(8):
                mi = cnt["m"] % 2
                cnt["m"] += 1
                bk, Bk = zchunk_fm(hT_, B_hT_, C_MA + i * 128, NT)
                act(sma[mi][:, 0:NT], bk[:, 0:NT], AF.Sigmoid, [Bk], [B_sma[mi]])
                bk, Bk = zchunk_fm(hT_, B_hT_, C_MB + i * 128, NT)
                act(smb[mi][:, 0:NT], bk[:, 0:NT], AF.Sigmoid, [Bk], [B_smb[mi]])
                bka, Bka = nextbank()
                for c in range(4):
                    mm(bka[:, 0:NT], w_ba_bf[:, c, i * 128:(i + 1) * 128], sga_[:, c, 0:NT], c == 0, c == 3,
                       [bf("w_ba"), B_sga_], [Bka])
                bkb, Bkb = nextbank()
                for c in range(4):
                    mm(bkb[:, 0:NT], w_bb_bf[:, c, i * 128:(i + 1) * 128], ug_[:, c, 0:NT], c == 0, c == 3,
                       [bf("w_bb"), B_ug_], [Bkb])
                tten("dve", t1[mi][:, 0:NT], bka[:, 0:NT], sma[mi][:, 0:NT], ALU.mult, [Bka, B_sma[mi]], [B_t1[mi]])
                tten("dve", t2[mi][:, 0:NT], bkb[:, 0:NT], smb[mi][:, 0:NT], ALU.mult, [Bkb, B_smb[mi]], [B_t2[mi]])
                tten("pool", mg_[:, i, 0:NT], t1[mi][:, 0:NT], t2[mi][:, 0:NT], ALU.add, [B_t1[mi], B_t2[mi]], [B_mg_])
            if b == 0 and not sample:
                dump("mergedT", mg_[:, :, :], [128, 8, 512], [B_mg_])
        osel = {}

        def rload(i):
            t0, P = tiles[i]
            if sample:
                dma("sp", xb[0][0:P, :], xs, [], [B_xb[0]])
                return
            dma("sp", xrl[i][:, :], xp[rb + t0:rb + t0 + P, :], [], [B_xrl[i]])

        def tailA(i):
            t0, P = tiles[i]
            oi = cnt["o"] % 2
            cnt["o"] += 1
            osel[i] = oi
            if sample:
                xt, Bxt = xb[0], B_xb[0]
            else:
                xt, Bxt = xrl[i], B_xrl[i]
            for hh in range(2):
                bk, Bk = nextbank()
                for k in range(8):
                    mm(bk[0:P, :], mg_[:, k, t0:t0 + P], w_out_bf[:, k, hh * 512:(hh + 1) * 512], k == 0, k == 7,
                       [B_mg_, bf("w_out")], [Bk])
                gsrc = gate_s[0:P, hh * 512:(hh + 1) * 512] if sample else gateB[:, hh * 512:(hh + 1) * 512]
                tten("dve", tt[hh][0:P, :], bk[0:P, :], gsrc, ALU.mult, [Bk, bf("gate_s" if sample else "gateB")], [B_tt[hh]])
                tten("dve", xt[0:P, hh * 512:(hh + 1) * 512], xt[0:P, hh * 512:(hh + 1) * 512], tt[hh][0:P, :], ALU.add,
                     [Bxt, B_tt[hh]], [Bxt])
            rmsnorm_stats(xt[0:P, :], P, Bxt, oi, xn2[oi][0:P, :], B_xn2[oi], ss2, ms2, r2, "f")

        def tailB(i):
            t0, P = tiles[i]
            oi = osel[i]
            if sample:
                xt, Bxt = xb[0], B_xb[0]
            else:
                xt, Bxt = xrl[i], B_xrl[i]
            stt(xt[0:P, :], xt[0:P, :], r2[oi][0:P, 0:1], fg_bc[0:P, :], ALU.mult, ALU.mult,
                [Bxt, bf(f"fr{oi}"), bf("fg")], [Bxt])
            if sample:
                dma("sp", ys, xt[0:P, :], [Bxt], [bf("o_ys")])
            else:
                dma("sp", yp[rb + t0:rb + t0 + P, :], xt[:, :], [Bxt], [bf(f"o_yp{i}")])

        return {"prenorm": prenorm, "xload": xload, "rload": rload, "xpose": xpose, "xposeT": xposeT,
                "xposeE": xposeE, "body": body,
                "bodygen": _body, "prefix": prefix,
                "tailA": tailA, "tailB": tailB, "n": len(tiles)}

    blocks = [mk_block(b, False) for b in range(4)] + [mk_block(0, True)]
    b0 = blocks[0]
    setup_a()
    build_wq()
    setup_b()
    b0["prenorm"](0)
    b0["prenorm"](1)
    b0["xload"](2)
    b0["xload"](3)
    b0["xposeT"](0)
    b0["xposeT"](1)
    build_wq2()
    setup_b2()
    b0["xposeE"](0)
    b0["prenorm"](2)
    b0["xposeT"](2)
    b0["xposeE"](1)
    b0["prenorm"](3)
    b0["xposeT"](3)
    b0["xposeE"](2)
    b0["xposeE"](3)
    SB_ = blocks[4]
    for bi in range(4):
        cur = blocks[bi]
        nxt = blocks[bi + 1] if bi < 3 else None

        def pre_merge(cur=cur, nxt=nxt, bi=bi):
            if bi == 0:
                gate_part()
                w_out_dma()
            for i in range(2):
                cur["rload"](i)
            if nxt is not None:
                for i in range(2):
                    nxt["prenorm"](i)
                for i in range(2, 4):
                    nxt["xload"](i)
            else:
                SB_["rload"](0)
        if bi < 3:
            if bi == 1:
                g1 = cur["bodygen"](pre_merge)
                next(g1)
                SB_["prenorm"](0)
                next(g1)
                next(g1)
                SB_["xpose"](0)
                next(g1, None)
            elif bi == 0:
                g0_ = cur["bodygen"](pre_merge)
                next(g0_)
                s1_sample()
                small_prep()
                for _ in g0_:
                    pass
            else:
                cur["body"](pre_merge)
            cur["rload"](2)
            cur["rload"](3)
            order = [("x", 0), ("x", 1), ("p", 2), ("p", 3), ("A", 0), ("A", 1), ("x", 2), ("B", 0), ("A", 2), ("x", 3),
                     ("B", 1), ("A", 3), ("B", 2), ("B", 3)]
            for kind, i in order:
                if kind == "x":
                    nxt["xpose"](i)
                elif kind == "p":
                    nxt["prenorm"](i)
                elif kind == "ps":
                    SB_["prenorm"](i)
                elif kind == "xs":
                    SB_["xpose"](i)
                elif kind == "A":
                    cur["tailA"](i)
                else:
                    cur["tailB"](i)
        else:
            SB_["prefix"]()
            g3 = cur["bodygen"](pre_merge)
            gs = SB_["bodygen"](None)
            for _ in range(3):
                next(g3)
                next(gs)
            next(g3, None)
            cur["rload"](2)
            cur["rload"](3)
            next(gs, None)
            for kind, i in [("A", 0), ("AS", 0), ("B", 0), ("A", 1), ("BS", 0), ("A", 2), ("B", 1), ("A", 3), ("B", 2), ("B", 3)]:
                if kind == "A":
                    cur["tailA"](i)
                elif kind == "B":
                    cur["tailB"](i)
                elif kind == "AS":
                    SB_["tailA"](i)
                else:
                    SB_["tailB"](i)
    S.emit()
    return nc, dbg_out


def make_in_maps(inp):
    f = lambda a: np.ascontiguousarray(np.asarray(a, dtype=np.float32))
    ident, band, invc, tril_t, sel = _consts()
    x_prompt, x_sample = f(inp["x_prompt"]), f(inp["x_sample"])
    state_pool, c_prompt, c_sample = f(inp["state_pool"]), f(inp["c_prompt"]), f(inp["c_sample"])
    sgu_w = f(inp["sgu_w"])[0]
    shared = {
        "w_ada": f(inp["w_ada"])[0],
        "b_ada": f(inp["b_ada"]).reshape(1, 3 * D),
        "ng_l": f(f(inp["norm_gain"]).reshape(8, 128).T),
        "w_in": f(inp["w_in"])[0],
        "pw_l": f(f(inp["pool_w"])[0].transpose(1, 0, 2).reshape(128, 512)),
        "pb_l": f(f(inp["pool_b"])[0].T),
        "psc_l": f(f(inp["pool_scale"]).reshape(4, 128).T),
        "psc_row": f(inp["pool_scale"]).reshape(1, 512),
        "lng_row": f(inp["sgu_ln_g"]).reshape(1, 512),
        "lnb_row": f(inp["sgu_ln_b"]).reshape(1, 512),
        "swt_l": f(sgu_w.transpose(2, 0, 1).reshape(128, 512)),
        "sgb_row": f(inp["sgu_b"]).reshape(1, 512),
        "w00_row": f(sgu_w[:, 0, 0]).reshape(1, 4),
        "b0_row": f(f(inp["sgu_b"])[0][:, 0]).reshape(1, 4),
        "w_ba": f(inp["w_branch_a"])[0],
        "w_bb": f(inp["w_branch_b"])[0],
        "w_out": f(inp["w_out"])[0],
        "fg_row": f(inp["final_gain"]).reshape(1, D),
        "k_ident": ident, "k_band": band, "k_invc": invc, "k_tril": tril_t, "k_sel": sel,
    }
    maps = []
    for c in range(8):
        m = dict(shared)
        m["xp"] = f(x_prompt[c])
        m["xs"] = f(x_sample[c * NS:(c + 1) * NS, 0])
        m["sp"] = f(state_pool[0, c * NS:(c + 1) * NS])
        m["cs"] = f(c_sample[c * NS:(c + 1) * NS])
        m["cp"] = f(c_prompt[c:c + 1])
        maps.append(m)
    return maps


_NC = None


def kernel(**inputs):
    global _NC
    if _NC is None:
        _NC = build()[0]
    maps = make_in_maps(inputs)
    res = run_bass_kernel_spmd(_NC, maps, core_ids=list(range(8)))
    rs = res.results
    y_prompt = np.stack([rs[c]["yp"] for c in range(8)], 0).astype(np.float32)
    y_sample = np.concatenate([rs[c]["ys"] for c in range(8)], 0).reshape(128, 1, D).astype(np.float32)
    npp = np.stack([rs[c]["npp"][1:16] for c in range(8)], 0)[None].astype(np.float32)
    nps = np.concatenate([rs[c]["nps"] for c in range(8)], 0)[None].astype(np.float32)
    nvp = np.stack([rs[c]["nvp"] for c in range(8)], 0)[None].astype(np.float32)
    nvs = np.concatenate([rs[c]["nvs"] for c in range(8)], 0).reshape(1, 128, 1, DA).astype(np.float32)
    return (y_prompt, y_sample, npp, nps, nvp, nvs)
```
